# Optimizing a Trainium2 kernel written in Bass

```python
import math
import jax, jax.numpy as jnp
from jax import lax
import numpy as np

D_MODEL = 2048
BATCH = 4
SEQ = 8192
DEPTH = 4

HEAD_DIM = 128
D_MIX = D_MODEL
N_GROUPS = 4
GW = D_MIX // N_GROUPS
GH = GW // HEAD_DIM
D_FF = 11 * D_MODEL // 4
DILATED_BRANCHES = ((128, 1), (512, 4), (2048, 16))
Q_BLOCK = 128
CHUNK = 64
CONV_WIDTH = 4
N_MOD = 9
EPS = 1e-6
NEG_BIG = -1e30
LB_FLOOR = 1e-30
IN_SIZES = (GW,) * 10 + (3 * GW, GW, GH, GH)
IN_COLS = 14 * GW + 2 * GH

kernel_name = 'hybrid_parallel_group_decoder'


def _split_points(sizes):
    pts, acc = [], 0
    for s in sizes[:-1]:
        acc += s
        pts.append(acc)
    return pts


def rmsnorm(x, gain):
    xf = x.astype(jnp.float32)
    y = xf * lax.rsqrt(jnp.mean(xf * xf, axis=-1, keepdims=True) + EPS)
    return (y * gain.astype(jnp.float32)).astype(x.dtype)


def modulate(h, shift, scale):
    return h * (1.0 + scale) + shift


def swiglu(h, w13, w2):
    gate, up = jnp.split(h @ w13, 2, axis=-1)
    return (jax.nn.silu(gate) * up) @ w2


def heads(t):
    B, S, W = t.shape
    return t.reshape(B, S, W // HEAD_DIM, HEAD_DIM).transpose(0, 2, 1, 3)


def merge_heads(t):
    B, H, S, Dh = t.shape
    return t.transpose(0, 2, 1, 3).reshape(B, S, H * Dh)


def head_rmsnorm(t, gain):
    B, S, W = t.shape
    th = t.reshape(B, S, W // HEAD_DIM, HEAD_DIM)
    th = th * lax.rsqrt(jnp.mean(th * th, axis=-1, keepdims=True) + EPS)
    return th.reshape(B, S, W) * gain.astype(jnp.float32)


def l2norm(t):
    return t * lax.rsqrt(jnp.sum(t * t, axis=-1, keepdims=True) + EPS)


def causal_depthwise_conv(x, w):
    C = x.shape[-1]
    return lax.conv_general_dilated(
        x, w[:, None, :], window_strides=(1,), padding=[(w.shape[0] - 1, 0)],
        dimension_numbers=('NWC', 'WIO', 'NWC'), feature_group_count=C)


def _to_blocks(t, n, blk):
    B, H, S, D = t.shape
    return t.reshape(B, H, n, blk, D).transpose(2, 0, 1, 3, 4)


def _from_blocks(t):
    n, B, H, blk, D = t.shape
    return t.transpose(1, 2, 0, 3, 4).reshape(B, H, n * blk, D)


def dilated_attention(q, k, v):
    B, H, S, Dh = q.shape
    nb = S // Q_BLOCK
    scale = Dh ** -0.5

    def one_block(args):
        blk, q_blk = args
        t = blk * Q_BLOCK + jnp.arange(Q_BLOCK)
        lses, outs = [], []
        for window, dil in DILATED_BRANCHES:
            offs = dil * jnp.arange(window // dil + 1)
            pos = t[:, None] - offs[None, :]
            valid = pos >= 0
            pos = jnp.maximum(pos, 0)
            kg = k[:, :, pos]
            vg = v[:, :, pos]
            s = jnp.einsum('bhqd,bhqwd->bhqw', q_blk, kg) * scale
            s = jnp.where(valid, s, NEG_BIG)
            m = jnp.max(s, axis=-1, keepdims=True)
            p = jnp.where(valid, jnp.exp(s - m), 0.0)
            l = jnp.sum(p, axis=-1, keepdims=True)
            outs.append(jnp.einsum('bhqw,bhqwd->bhqd', p, vg) / l)
            lses.append(m + jnp.log(l))
        wts = jax.nn.softmax(jnp.stack(lses), axis=0)
        return jnp.sum(wts * jnp.stack(outs), axis=0)

    o = lax.map(one_block, (jnp.arange(nb), _to_blocks(q, nb, Q_BLOCK)))
    return _from_blocks(o)


def stick_breaking_attention(q, k, v):
    B, H, S, Dh = q.shape
    nb = S // Q_BLOCK
    scale = Dh ** -0.5
    key_pos = jnp.arange(S)

    def one_block(args):
        blk, q_blk = args
        t = blk * Q_BLOCK + jnp.arange(Q_BLOCK)
        z = jnp.einsum('bhqd,bhsd->bhqs', q_blk, k) * scale
        causal = key_pos[None, :] < t[:, None]
        log_keep = jnp.where(causal, jax.nn.log_sigmoid(-z), 0.0)
        tail = lax.cumsum(log_keep, axis=3, reverse=True)
        a = jnp.where(causal, jnp.exp(jnp.where(causal, z + tail, NEG_BIG)), 0.0)
        return jnp.einsum('bhqs,bhsd->bhqd', a, v)

    o = lax.map(one_block, (jnp.arange(nb), _to_blocks(q, nb, Q_BLOCK)))
    return _from_blocks(o)


def hgrn2_recurrence(q, k, v, log_f):
    B, H, S, Dk = q.shape
    n = S // CHUNK
    incl = jnp.tril(jnp.ones((CHUNK, CHUNK), dtype=bool))[:, :, None]

    def step(state, inp):
        qc, kc, vc, lf = inp
        b = jnp.cumsum(lf, axis=2)
        rel = b[:, :, :, None, :] - b[:, :, None, :, :]
        decay = jnp.where(incl, jnp.exp(jnp.where(incl, rel, 0.0)), 0.0)
        scores = jnp.einsum('bhtc,bhtsc,bhsc->bhts', qc, decay, kc)
        o = (jnp.einsum('bhtc,bhcv->bhtv', qc * jnp.exp(b), state)
             + jnp.einsum('bhts,bhsv->bhtv', scores, vc))
        b_last = b[:, :, -1:, :]
        state = (jnp.exp(b[:, :, -1, :])[..., None] * state
                 + jnp.einsum('bhsc,bhsv->bhcv', kc * jnp.exp(b_last - b), vc))
        return state, o

    state0 = jnp.zeros((B, H, Dk, v.shape[-1]), jnp.float32)
    _, o = lax.scan(step, state0, (_to_blocks(q, n, CHUNK), _to_blocks(k, n, CHUNK),
                                   _to_blocks(v, n, CHUNK), _to_blocks(log_f, n, CHUNK)))
    return _from_blocks(o)


def gated_delta_rule(q, k, v, log_alpha, beta):
    B, H, S, Dk = q.shape
    n = S // CHUNK
    q = q * Dk ** -0.5

    def chunks(t):
        return t.reshape((B, H, n, CHUNK) + t.shape[3:])

    qc, kc, vc, bc = chunks(q), chunks(k), chunks(v), chunks(beta)
    g = jnp.cumsum(chunks(log_alpha), axis=-1)
    incl = jnp.tril(jnp.ones((CHUNK, CHUNK), dtype=bool))
    strict = jnp.tril(jnp.ones((CHUNK, CHUNK), dtype=bool), -1)
    rel = g[..., :, None] - g[..., None, :]
    decay = jnp.where(incl, jnp.exp(jnp.where(incl, rel, 0.0)), 0.0)
    k_beta = kc * bc[..., None]
    kk = jnp.einsum('bhntd,bhnsd->bhnts', k_beta, kc) * decay
    lhs = jnp.eye(CHUNK, dtype=jnp.float32) + jnp.where(strict, kk, 0.0)
    u = lax.linalg.triangular_solve(lhs, vc * bc[..., None], left_side=True, lower=True, unit_diagonal=True)
    w = lax.linalg.triangular_solve(lhs, k_beta * jnp.exp(g)[..., None], left_side=True, lower=True, unit_diagonal=True)
    qk = jnp.einsum('bhntd,bhnsd->bhnts', qc, kc) * decay

    def step(state, inp):
        qi, ki, ui, wi, qki, gi = inp
        v_new = ui - jnp.einsum('bhtd,bhdv->bhtv', wi, state)
        o = (jnp.einsum('bhtd,bhdv->bhtv', qi * jnp.exp(gi)[..., None], state)
             + jnp.einsum('bhts,bhsv->bhtv', qki, v_new))
        g_last = gi[..., -1:]
        state = (jnp.exp(g_last)[..., None] * state
                 + jnp.einsum('bhsd,bhsv->bhdv', ki * jnp.exp(g_last - gi)[..., None], v_new))
        return state, o

    state0 = jnp.zeros((B, H, Dk, v.shape[-1]), jnp.float32)
    xs = tuple(jnp.moveaxis(t, 2, 0) for t in (qc, kc, u, w, qk, g))
    _, o = lax.scan(step, state0, xs)
    return _from_blocks(o)


def hybrid_mixer(h, w_in, w_out, out_gain, lower_bound, conv_w, a_log, dt_bias):
    f32 = jnp.float32
    proj = (h @ w_in).astype(f32)
    (a_q, a_k, a_v, b_q, b_k, b_v, c_q, c_f, c_i, c_g,
     d_qkv, d_z, d_a, d_b) = jnp.split(proj, _split_points(IN_SIZES), axis=-1)
    o_a = head_rmsnorm(merge_heads(dilated_attention(heads(a_q), heads(a_k), heads(a_v))), out_gain[0])
    o_b = head_rmsnorm(merge_heads(stick_breaking_attention(heads(b_q), heads(b_k), heads(b_v))), out_gain[1])
    lb = lower_bound.astype(f32)
    log_f = jnp.logaddexp(jnp.log(jnp.maximum(lb, LB_FLOOR)), jnp.log1p(-lb) + jax.nn.log_sigmoid(c_f))
    c_k = (1.0 - lb) * jax.nn.sigmoid(-c_f)
    o_c = hgrn2_recurrence(heads(jax.nn.silu(c_q)), heads(c_k), heads(c_i), heads(log_f))
    o_c = head_rmsnorm(merge_heads(o_c), out_gain[2]) * jax.nn.silu(c_g)
    d_qkv = jax.nn.silu(causal_depthwise_conv(d_qkv, conv_w.astype(f32)))
    d_q, d_k, d_v = jnp.split(d_qkv, 3, axis=-1)
    beta = jax.nn.sigmoid(d_b).transpose(0, 2, 1)
    log_alpha = (-jnp.exp(a_log.astype(f32)) * jax.nn.softplus(d_a + dt_bias.astype(f32))).transpose(0, 2, 1)
    o_d = gated_delta_rule(l2norm(heads(d_q)), l2norm(heads(d_k)), heads(d_v), log_alpha, beta)
    o_d = head_rmsnorm(merge_heads(o_d), out_gain[3]) * jax.nn.silu(d_z)
    mixed = jnp.concatenate([o_a, o_b, o_c, o_d], axis=-1).astype(h.dtype)
    return mixed @ w_out


def setup_inputs(seed: int = 0) -> dict:
    key = jax.random.key(seed)
    ks = jax.random.split(key, 16)
    nrm = jax.random.normal
    x = nrm(ks[0], (BATCH, SEQ, D_MODEL), jnp.float32)
    c = nrm(ks[1], (BATCH, D_MODEL), jnp.float32)
    w_mod = nrm(ks[2], (DEPTH, D_MODEL, N_MOD * D_MODEL), jnp.float32) * (0.5 * D_MODEL ** -0.5)
    b_mod = 0.02 * nrm(ks[3], (DEPTH, N_MOD * D_MODEL), jnp.float32)
    norm_gain = 1.0 + 0.05 * nrm(ks[4], (DEPTH, 6, D_MODEL), jnp.float32)
    w_in = nrm(ks[5], (DEPTH, D_MODEL, IN_COLS), jnp.float32) * D_MODEL ** -0.5
    w_out = nrm(ks[6], (DEPTH, D_MIX, D_MODEL), jnp.float32) * D_MIX ** -0.5
    mix_out_gain = 1.0 + 0.05 * nrm(ks[7], (DEPTH, N_GROUPS, GW), jnp.float32)
    hgrn_lb_logits = 0.5 * nrm(ks[8], (DEPTH, GW), jnp.float32)
    dn_conv_w = nrm(ks[9], (DEPTH, CONV_WIDTH, 3 * GW), jnp.float32) * CONV_WIDTH ** -0.5
    dn_a_log = jnp.log(jax.random.uniform(ks[10], (DEPTH, GH), jnp.float32, 1.0, 16.0))
    dt = jnp.exp(jax.random.uniform(ks[11], (DEPTH, GH), jnp.float32, math.log(1e-3), math.log(1e-1)))
    dn_dt_bias = dt + jnp.log(-jnp.expm1(-dt))
    ffn1_w13 = nrm(ks[12], (DEPTH, D_MODEL, 2 * D_FF), jnp.float32) * D_MODEL ** -0.5
    ffn1_w2 = nrm(ks[13], (DEPTH, D_FF, D_MODEL), jnp.float32) * D_FF ** -0.5
    ffn2_w13 = nrm(ks[14], (DEPTH, D_MODEL, 2 * D_FF), jnp.float32) * D_MODEL ** -0.5
    ffn2_w2 = nrm(ks[15], (DEPTH, D_FF, D_MODEL), jnp.float32) * D_FF ** -0.5
    return {'x': x, 'c': c, 'w_mod': w_mod, 'b_mod': b_mod, 'norm_gain': norm_gain,
            'w_in': w_in, 'w_out': w_out, 'mix_out_gain': mix_out_gain,
            'hgrn_lb_logits': hgrn_lb_logits, 'dn_conv_w': dn_conv_w, 'dn_a_log': dn_a_log,
            'dn_dt_bias': dn_dt_bias, 'ffn1_w13': ffn1_w13, 'ffn1_w2': ffn1_w2,
            'ffn2_w13': ffn2_w13, 'ffn2_w2': ffn2_w2}


def reference(x, c, w_mod, b_mod, norm_gain, w_in, w_out, mix_out_gain, hgrn_lb_logits,
              dn_conv_w, dn_a_log, dn_dt_bias, ffn1_w13, ffn1_w2, ffn2_w13, ffn2_w2):
    lb_p = jax.nn.softmax(hgrn_lb_logits.astype(jnp.float32), axis=0)
    lower_bounds = jnp.cumsum(lb_p, axis=0) - lb_p[0]
    cond = jax.nn.silu(c)
    for l in range(DEPTH):
        mod = (cond @ w_mod[l] + b_mod[l])[:, None, :]
        sh1, sc1, g1, sh2, sc2, g2, sh3, sc3, g3 = jnp.split(mod, N_MOD, axis=-1)
        h = modulate(rmsnorm(x, norm_gain[l, 0]), sh1, sc1)
        x = x + (0.5 * g1 * rmsnorm(swiglu(h, ffn1_w13[l], ffn1_w2[l]), norm_gain[l, 1])).astype(x.dtype)
        h = modulate(rmsnorm(x, norm_gain[l, 2]), sh2, sc2)
        y = hybrid_mixer(h, w_in[l], w_out[l], mix_out_gain[l], lower_bounds[l],
                         dn_conv_w[l], dn_a_log[l], dn_dt_bias[l])
        x = x + (g2 * rmsnorm(y, norm_gain[l, 3])).astype(x.dtype)
        h = modulate(rmsnorm(x, norm_gain[l, 4]), sh3, sc3)
        x = x + (0.5 * g3 * rmsnorm(swiglu(h, ffn2_w13[l], ffn2_w2[l]), norm_gain[l, 5])).astype(x.dtype)
    return x
```

```python
import contextlib
import numpy as np
import ml_dtypes
import concourse.bass as bass
import concourse.mybir as mybir
from concourse.bass_utils import run_bass_kernel_spmd
from concourse.alu_op_type import AluOpType as ALU

F32 = mybir.dt.float32
BF16 = mybir.dt.bfloat16
I32 = mybir.dt.int32
AF = mybir.ActivationFunctionType
AX = mybir.AxisListType

D_MODEL = 2048
BATCH = 4
SEQ = 8192
DEPTH = 4
HD = 128
GW = 512
GH = 4
D_FF = 5632
NFF = D_FF // 128
KC = D_MODEL // 128
IN_COLS = 14 * GW + 2 * GH
N_MOD = 9
EPS = 1e-6
NCORES = 8
TOK_CORE = BATCH * SEQ // NCORES
TT = 512
NT = TOK_CORE // TT


class Buf:
    __slots__ = ("name", "w", "r", "excl")

    def __init__(self, name, excl=False):
        self.name = name
        self.excl = excl
        self.w = None
        self.r = []


class Eng:
    def __init__(self, k, name, handle, sem, selfsync=True):
        self.k = k
        self.name = name
        self.h = handle
        self.sem = sem
        self.count = 0
        self.waited = {}
        self.selfsync = selfsync

    def wait(self, ev):
        if ev is None:
            return
        sem, val, owner, epoch = ev
        if epoch != self.k.epoch:
            return
        if owner is self and not self.selfsync:
            return
        key = id(sem)
        if self.waited.get(key, 0) >= val:
            return
        self.waited[key] = val
        self.h.wait_ge(sem, val)


class K:
    def __init__(self, nc, es):
        self.nc = nc
        self.es = es
        self.nsem = 0
        self.epoch = 0
        self.tes = es
        self.free_slots = []
        self.live_slots = []
        self.marks = []
        self.bar1 = self.newsem("bar1")
        self.bar2 = self.newsem("bar2")
        self.nbar = 0
        self.pe = Eng(self, "pe", nc.tensor, self.newsem("pe"), selfsync=False)
        self.act = Eng(self, "act", nc.scalar, self.newsem("act"))
        self.dve = Eng(self, "dve", nc.vector, self.newsem("dve"))
        self.pool = Eng(self, "pool", nc.gpsimd, self.newsem("pool"))
        self.sp = Eng(self, "sp", nc.sync, self.newsem("sp"))
        self.engs = [self.pe, self.act, self.dve, self.pool, self.sp]
        self.dma_sems = []
        self.ndma = 0
        self.ninst = 0

    def newsem(self, name):
        self.nsem += 1
        return self.es.enter_context(self.nc.semaphore(f"{name}_{self.nsem}"))

    def sb(self, name, shape, dt):
        self.nsb = getattr(self, "nsb", 0) + 1
        return self.tes.enter_context(self.nc.sbuf_tensor(f"sb{self.nsb}_{name}", shape, dt))

    def _deps(self, eng, reads, writes):
        for b in reads:
            eng.wait(b.w)
            if b.excl:
                for ev in b.r:
                    eng.wait(ev)
        for b in writes:
            eng.wait(b.w)
            for ev in b.r:
                eng.wait(ev)

    def _commit(self, ev, reads, writes):
        for b in reads:
            b.r.append(ev)
            if len(b.r) > 24:
                b.r = b.r[-24:]
        for b in writes:
            b.w = ev
            b.r = []

    def op(self, eng, fn, reads=(), writes=()):
        self._deps(eng, reads, writes)
        ins = fn(eng.h)
        eng.count += 1
        ins.then_inc(eng.sem, 1)
        ev = (eng.sem, eng.count, eng, self.epoch)
        self._commit(ev, reads, writes)
        self.ninst += 1
        return ev

    def group(self, eng, fns, reads=(), writes=()):
        self._deps(eng, reads, writes)
        ins = None
        for fn in fns:
            ins = fn(eng.h)
            self.ninst += 1
        eng.count += 1
        ins.then_inc(eng.sem, 1)
        ev = (eng.sem, eng.count, eng, self.epoch)
        self._commit(ev, reads, writes)
        return ev

    def dma(self, q, slot, out, in_, reads=(), writes=(), **kw):
        self._deps(q, reads, writes)
        ins = q.h.dma_start(out=out, in_=in_, **kw)
        slot.count += 16
        ins.then_inc(slot.sem, 16)
        ev = (slot.sem, slot.count, None, self.epoch)
        self._commit(ev, reads, writes)
        self.ndma += 1
        return ev

    def slot(self, name):
        if self.free_slots:
            s = self.free_slots.pop()
        else:
            s = Slot(self.newsem(name))
            self.dma_sems.append(s)
        self.live_slots.append(s)
        return s

    def enter(self, tes):
        self.tes = tes
        self.marks.append(len(self.live_slots))

    def leave(self, prev):
        m = self.marks.pop()
        self.free_slots.extend(self.live_slots[m:])
        del self.live_slots[m:]
        self.tes = prev

    def barrier(self):
        evs = [(e.sem, e.count, None, self.epoch) for e in self.engs if e.count > 0]
        evs += [(s.sem, s.count, None, self.epoch) for s in self.dma_sems if s.count > 0]
        for e in self.engs:
            for ev in evs:
                if ev[0] is e.sem:
                    continue
                e.wait(ev)

    def reset(self):
        self.barrier()
        return
        self.nbar += 1
        n = len(self.engs) * self.nbar
        for e in self.engs:
            e.h.sem_inc(self.bar1, 1)
        for e in self.engs:
            e.h.wait_ge(self.bar1, n)
            e.h.sem_clear(e.sem)
            if e is self.sp:
                for s in self.dma_sems:
                    e.h.sem_clear(s.sem)
            e.h.sem_inc(self.bar2, 1)
        for e in self.engs:
            e.h.wait_ge(self.bar2, n)
            e.count = 0
            e.waited = {}
        for s in self.dma_sems:
            s.count = 0
        self.epoch += 1

    def finish(self, evs):
        for ev in evs:
            self.sp.wait(ev)


class Slot:
    def __init__(self, sem):
        self.sem = sem
        self.count = 0


class Ring:
    def __init__(self, items):
        self.items = items
        self.i = 0

    def next(self):
        it = self.items[self.i % len(self.items)]
        self.i += 1
        return it


class Ctx:
    pass


def make_consts():
    c = {}
    ident = np.eye(128, dtype=np.float32)
    c["ident"] = ident
    return c


def setup_common(k, nc, es, consts_ap):
    S = Ctx()
    S.k, S.nc = k, nc
    S.ps = [es.enter_context(nc.psum_tensor(f"psb{i}", [128, 512], F32)) for i in range(8)]
    S.psb = [Buf(f"psb{i}", excl=True) for i in range(8)]
    S.psi = 0
    S.ident_f = k.sb("ident_f", [128, 128], F32)
    S.ident_b = k.sb("ident_b", [128, 128], BF16)
    S.b_const = Buf("const")
    sl = k.slot("const")
    k.dma(k.sp, sl, S.ident_f[:], consts_ap["ident"], writes=[S.b_const])
    k.op(k.dve, lambda e: e.tensor_copy(out=S.ident_b[:], in_=S.ident_f[:]), reads=[S.b_const], writes=[S.b_const])
    return S


def bank(S):
    i = S.psi % 8
    S.psi += 1
    return S.ps[i], S.psb[i]


def rstd_from_ss(S, ss, b_ss, n, tag):
    k = S.k
    k.op(k.act, lambda e: e.activation(out=ss, in_=ss, func=AF.Sqrt, scale=1.0 / n, bias=S.eps_col[:, 0:1]),
         reads=[b_ss, S.b_const], writes=[b_ss])
    k.op(k.dve, lambda e: e.reciprocal(out=ss, in_=ss), reads=[b_ss], writes=[b_ss])


def setup_row(S, es):
    k = S.k
    S.hT = k.sb("hT", [128, KC * TT], BF16)
    S.b_hT = Buf("hT")
    S.GT = k.sb("GT", [128, NFF * TT], BF16)
    S.b_GT = [Buf(f"GT{j}") for j in range(NFF)]
    S.w13 = Ring([(k.sb(f"w13_{i}", [128, KC * 256], BF16), Buf(f"w13_{i}"), k.slot(f"w13_{i}")) for i in range(3)])
    S.w2 = Ring([(k.sb(f"w2_{i}", [128, 4 * 512], BF16), Buf(f"w2_{i}"), k.slot(f"w2_{i}")) for i in range(3)])
    S.y = k.sb("y", [128, 4 * D_MODEL], F32)
    S.b_y = [Buf(f"y{i}") for i in range(4)]
    S.xr = Ring([(k.sb(f"x_{i}", [128, D_MODEL], F32), Buf(f"x_{i}"), k.slot(f"x_{i}")) for i in range(2)])
    S.xn = Ring([(k.sb(f"xn_{i}", [128, D_MODEL], BF16), Buf(f"xn_{i}")) for i in range(2)])
    S.junk = k.sb("junk", [128, D_MODEL], BF16)
    S.b_junk = Buf("junk")
    S.sg = Ring([(k.sb(f"sg_{i}", [128, TT], F32), Buf(f"sg_{i}")) for i in range(2)])
    S.grow = k.sb("grow", [128, D_MODEL], F32)
    S.b_grow = Buf("grow")
    S.tmp = k.sb("tmp", [128, D_MODEL], F32)
    S.b_tmp = Buf("tmp")
    S.ss = Ring([(k.sb(f"ss_{i}", [128, 1], F32), Buf(f"ss_{i}")) for i in range(4)])
    S.modP = k.sb("modP", [128, N_MOD * KC], F32)
    S.gainP = k.sb("gainP", [128, 6 * KC], F32)
    S.coefA = k.sb("coefA", [128, KC], F32)
    S.b_mod = Buf("mod")
    S.b_coef = Buf("coef")
    S.s_mod = k.slot("mod")
    S.s_grow = k.slot("grow")
    S.eps_col = k.sb("eps_col", [128, 1], F32)
    k.op(k.dve, lambda e: e.memset(S.eps_col[:], EPS), writes=[S.b_const])


def load_mod(S, mod_ap, gain_ap, b_moddram):
    k = S.k
    k.dma(k.sp, S.s_mod, S.modP[:].rearrange("p (j c) -> p j c", c=KC),
          mod_ap.rearrange("j (c p) -> p j c", p=128), reads=[b_moddram], writes=[S.b_mod],
          allow_slow_non_contiguous=True)
    k.dma(k.sp, S.s_mod, S.gainP[:].rearrange("p (j c) -> p j c", c=KC),
          gain_ap.rearrange("j (c p) -> p j c", p=128), reads=[b_moddram], writes=[S.b_mod],
          allow_slow_non_contiguous=True)


def set_stage_coefs(S, mod_ap, gain_ap, b_moddram, n_pre, j_sh, j_sc, j_g, n_post, gscale):
    k = S.k
    k.op(k.dve, lambda e: e.scalar_tensor_tensor(
        out=S.coefA[:], in0=S.modP[:, j_sc * KC:(j_sc + 1) * KC], scalar=1.0,
        in1=S.gainP[:, n_pre * KC:(n_pre + 1) * KC], op0=ALU.add, op1=ALU.mult),
        reads=[S.b_mod], writes=[S.b_coef])
    S.shift = S.modP[:, j_sh * KC:(j_sh + 1) * KC]
    if j_g is not None:
        k.dma(k.sp, S.s_grow, S.grow[:], mod_ap[j_g:j_g + 1, :].partition_broadcast(128),
              reads=[b_moddram], writes=[S.b_grow])
        k.dma(k.sp, S.s_grow, S.tmp[:], gain_ap[n_post:n_post + 1, :].partition_broadcast(128),
              reads=[b_moddram], writes=[S.b_tmp])
        k.op(k.dve, lambda e: e.scalar_tensor_tensor(
            out=S.grow[:], in0=S.grow[:], scalar=float(gscale), in1=S.tmp[:], op0=ALU.mult, op1=ALU.mult),
            reads=[S.b_grow, S.b_tmp], writes=[S.b_grow])


def prep_tile(S, x_rows, b_x):
    k = S.k
    for tb in range(TT // 128):
        xt, b_xt, s_xt = S.xr.next()
        k.dma(k.sp, s_xt, xt[:], x_rows(tb), reads=[b_x], writes=[b_xt])
        ss, b_ss = S.ss.next()
        k.op(k.act, lambda e: e.activation(out=S.junk[:], in_=xt[:], func=AF.Square, accum_out=ss[:]),
             reads=[b_xt], writes=[S.b_junk, b_ss])
        rstd_from_ss(S, ss[:], b_ss, D_MODEL, "p")
        xn, b_xn = S.xn.next()
        k.op(k.act, lambda e: e.activation(out=xn[:], in_=xt[:], func=AF.Copy, scale=ss[:, 0:1]),
             reads=[b_xt, b_ss], writes=[b_xn])
        for half in range(2):
            pt, b_pt = bank(S)
            ptb = pt[:].bitcast(BF16)
            fns = []
            for i in range(8):
                fc = half * 8 + i
                fns.append(lambda e, i=i, fc=fc: e.transpose(
                    out=ptb[:, i * 128:(i + 1) * 128], in_=xn[:, fc * 128:(fc + 1) * 128], identity=S.ident_b[:]))
            k.group(k.pe, fns, reads=[b_xn, S.b_const], writes=[b_pt])
            for i in range(8):
                fc = half * 8 + i
                dst = S.hT[:, fc * TT + tb * 128: fc * TT + (tb + 1) * 128]
                src = ptb[:, i * 128:(i + 1) * 128]
                if half == 0:
                    k.op(k.dve, lambda e, dst=dst, src=src, fc=fc: e.tensor_scalar(
                        out=dst, in0=src, scalar1=S.coefA[:, fc:fc + 1], scalar2=S.shift[:, fc:fc + 1],
                        op0=ALU.mult, op1=ALU.add), reads=[b_pt, S.b_coef, S.b_mod], writes=[S.b_hT])
                else:
                    k.op(k.act, lambda e, dst=dst, src=src, fc=fc: e.activation(
                        out=dst, in_=src, func=AF.Identity, scale=S.coefA[:, fc:fc + 1], bias=S.shift[:, fc:fc + 1]),
                        reads=[b_pt, S.b_coef, S.b_mod], writes=[S.b_hT])


def mm_tokmajor(S, lhs, b_lhs, nk, w_tile, b_w, y_dst, b_y):
    k = S.k
    ng = nk // 4
    for cb in range(4):
        banks = [bank(S) for _ in range(4)]
        for jg in range(ng):
            wt, b_wt, s_wt = S.w2.next()
            k.dma(k.sp, s_wt, wt[:], w_tile(cb, jg), reads=[b_w], writes=[b_wt])
            fns = []
            for jj in range(4):
                j = jg * 4 + jj
                for tb in range(4):
                    fns.append(lambda e, j=j, jj=jj, tb=tb: e.matmul(
                        banks[tb][0][:, :], lhsT=lhs(j)[:, tb * 128:(tb + 1) * 128],
                        rhs=wt[:, jj * 512:(jj + 1) * 512], start=(j == 0), stop=(j == nk - 1)))
            k.group(k.pe, fns, reads=[b_wt] + [b_lhs[jg * 4 + jj] for jj in range(4)],
                    writes=[b for _, b in banks])
        for tb in range(4):
            dst = y_dst(tb, cb)
            if tb % 2 == 0:
                k.op(k.act, lambda e, dst=dst, tb=tb: e.activation(out=dst, in_=banks[tb][0][:, :], func=AF.Copy),
                     reads=[banks[tb][1]], writes=[b_y[tb]])
            else:
                k.op(k.dve, lambda e, dst=dst, tb=tb: e.tensor_copy(out=dst, in_=banks[tb][0][:, :]),
                     reads=[banks[tb][1]], writes=[b_y[tb]])


def residual_epilogue(S, x_rows_in, b_xin, x_rows_out, b_xout, store_q=None):
    k = S.k
    evs = []
    for tb in range(4):
        ysl = S.y[:, tb * D_MODEL:(tb + 1) * D_MODEL]
        ss, b_ss = S.ss.next()
        k.op(k.act, lambda e: e.activation(out=S.junk[:], in_=ysl, func=AF.Square, accum_out=ss[:]),
             reads=[S.b_y[tb]], writes=[S.b_junk, b_ss])
        rstd_from_ss(S, ss[:], b_ss, D_MODEL, "e")
        xt, b_xt, s_xt = S.xr.next()
        k.dma(k.sp, s_xt, xt[:], x_rows_in(tb), reads=[b_xin], writes=[b_xt])
        k.op(k.dve, lambda e: e.scalar_tensor_tensor(
            out=S.tmp[:], in0=ysl, scalar=ss[:, 0:1], in1=S.grow[:], op0=ALU.mult, op1=ALU.mult),
            reads=[S.b_y[tb], b_ss, S.b_grow], writes=[S.b_tmp])
        k.op(k.pool, lambda e: e.tensor_tensor(out=xt[:], in0=S.tmp[:], in1=xt[:], op=ALU.add),
             reads=[S.b_tmp], writes=[b_xt])
        evs.append(k.dma(store_q or k.pool, s_xt, x_rows_out(tb), xt[:], reads=[b_xt], writes=[b_xout]))
    return evs


def ffn_tile(S, w13_tile, b_w13, w2_tile, b_w2):
    k = S.k
    for j in range(NFF):
        wt, b_wt, s_wt = S.w13.next()
        k.dma(k.sp, s_wt, wt[:], w13_tile(j), reads=[b_w13], writes=[b_wt])
        pg, b_pg = bank(S)
        pu, b_pu = bank(S)
        fns = []
        for kc in range(KC):
            fns.append(lambda e, kc=kc: e.matmul(pg[:, :], lhsT=wt[:, kc * 256: kc * 256 + 128],
                                                  rhs=S.hT[:, kc * TT:(kc + 1) * TT], start=(kc == 0), stop=(kc == KC - 1)))
        for kc in range(KC):
            fns.append(lambda e, kc=kc: e.matmul(pu[:, :], lhsT=wt[:, kc * 256 + 128: kc * 256 + 256],
                                                  rhs=S.hT[:, kc * TT:(kc + 1) * TT], start=(kc == 0), stop=(kc == KC - 1)))
        k.group(k.pe, fns, reads=[b_wt, S.b_hT], writes=[b_pg, b_pu])
        sg, b_sg = S.sg.next()
        k.op(k.act, lambda e: e.activation(out=sg[:], in_=pg[:, :], func=AF.Silu), reads=[b_pg], writes=[b_sg])
        k.op(k.dve, lambda e, j=j: e.tensor_tensor(out=S.GT[:, j * TT:(j + 1) * TT], in0=sg[:], in1=pu[:, :], op=ALU.mult),
             reads=[b_sg, b_pu], writes=[S.b_GT[j]])
    mm_tokmajor(S, lambda j: S.GT[:, j * TT:(j + 1) * TT], S.b_GT, NFF, w2_tile, b_w2,
                lambda tb, cb: S.y[:, tb * D_MODEL + cb * 512: tb * D_MODEL + (cb + 1) * 512], S.b_y)


def cast_copy(k, slot, dst_flat, src_flat, n, reads, writes):
    assert n % 2048 == 0
    rows = n // 2048
    d2 = dst_flat.rearrange("(r c) -> r c", c=2048)
    s2 = src_flat.rearrange("(r c) -> r c", c=2048)
    ev = None
    r0 = 0
    while r0 < rows:
        r1 = min(rows, r0 + 4096)
        ev = k.dma(k.pool, slot, d2[r0:r1, :], s2[r0:r1, :], reads=reads, writes=writes)
        r0 = r1
    return ev


def lay_w13(w13):
    g = w13[:, :D_FF].reshape(KC, 128, NFF, 128)
    u = w13[:, D_FF:].reshape(KC, 128, NFF, 128)
    t = np.stack([g, u], axis=3)
    return np.ascontiguousarray(t.transpose(2, 1, 0, 3, 4).reshape(NFF, 128, KC * 256))


def lay_w2(w2, nk):
    t = w2.reshape(nk // 4, 4, 128, 4, 512)
    return np.ascontiguousarray(t.transpose(3, 0, 2, 1, 4).reshape(4, nk // 4, 128, 4 * 512))


NFMB = 16
NFMF = 28
FM_COLS = ([0 + 128 * i for i in range(4)] + [512 + 128 * i for i in range(4)] +
           [1536 + 128 * i for i in range(4)] + [2048 + 128 * i for i in range(4)] +
           [3072 + 128 * i for i in range(4)] + [3584 + 128 * i for i in range(4)] +
           [4608 + 128 * i for i in range(4)] + [5120 + 128 * i for i in range(12)] +
           [6656 + 128 * i for i in range(4)])
TM_COLS = [1024, 2560, 4096]
AB_COL = 7168
QSCALE = HD ** -0.5


def lay_win(w_in):
    cols = [w_in[:, c:c + 128].reshape(KC, 128, 128) for c in FM_COLS]
    fm = np.stack(cols, axis=0).reshape(22, 2, KC, 128, 128)
    fm = np.ascontiguousarray(fm.transpose(0, 3, 2, 1, 4).reshape(22, 128, KC * 256))
    tm = []
    for c in TM_COLS:
        w = w_in[:, c:c + 512].reshape(4, 4, 128, 512)
        tm.append(w.transpose(0, 2, 1, 3).reshape(4, 128, 4 * 512))
    tm = np.ascontiguousarray(np.stack(tm, axis=0))
    ab = np.zeros((128, KC, 128), np.float32)
    ab[:, :, 0:8] = w_in[:, AB_COL:AB_COL + 8].reshape(KC, 128, 8).transpose(1, 0, 2)
    return fm, tm, np.ascontiguousarray(ab.reshape(128, KC * 128))


def setup_proj(S, es):
    k = S.k
    S.stb = Ring([(k.sb(f"stb_{i}", [128, TT], BF16), Buf(f"stb_{i}"), k.slot(f"stb_{i}")) for i in range(3)])
    S.stf = Ring([(k.sb(f"stf_{i}", [128, TT], F32), Buf(f"stf_{i}"), k.slot(f"stf_{i}")) for i in range(3)])
    S.vt = Ring([(k.sb(f"vt_{i}", [128, 3 * 512], BF16), Buf(f"vt_{i}"), k.slot(f"vt_{i}")) for i in range(4)])
    S.wab = k.sb("wab_sb", [128, KC * 128], BF16)
    S.b_wab = Buf("wab")
    S.s_wab = k.slot("wab")
    S.abst = k.sb("abst", [8, TT], F32)
    S.b_abst = Buf("abst")
    S.s_abst = k.slot("abst")


def proj_tile(S, t0, wfm_tile, b_wfm, wtm_tile, b_wtm, FMb, FMf, TMv, AB, b_pr):
    import os
    PSTOP = int(os.environ.get('PSTOP', '9'))
    k = S.k
    if PSTOP <= 0:
        return
    for pr in range(22):
        wt, b_wt, s_wt = S.w13.next()
        k.dma(k.sp, s_wt, wt[:], wfm_tile(pr), reads=[b_wfm], writes=[b_wt])
        for w in range(2):
            c = pr * 2 + w
            pb, b_pb = bank(S)
            fns = [(lambda e, kc=kc: e.matmul(pb[:, :], lhsT=wt[:, kc * 256 + w * 128: kc * 256 + (w + 1) * 128],
                                              rhs=S.hT[:, kc * TT:(kc + 1) * TT], start=(kc == 0), stop=(kc == KC - 1)))
                   for kc in range(KC)]
            k.group(k.pe, fns, reads=[b_wt, S.b_hT], writes=[b_pb])
            if c < NFMB:
                st, b_st, s_st = S.stb.next()
                sc = QSCALE if (c < 4 or 8 <= c < 12) else 1.0
                dst = FMb[c, :, t0:t0 + TT]
            else:
                st, b_st, s_st = S.stf.next()
                sc = 1.0
                dst = FMf[c - NFMB, :, t0:t0 + TT]
            if c % 2 == 0:
                k.op(k.act, lambda e: e.activation(out=st[:], in_=pb[:, :], func=AF.Copy, scale=float(sc)),
                     reads=[b_pb], writes=[b_st])
            else:
                k.op(k.dve, lambda e: e.tensor_scalar(out=st[:], in0=pb[:, :], scalar1=float(sc), scalar2=None, op0=ALU.mult),
                     reads=[b_pb], writes=[b_st])
            k.dma(k.pool, s_st, dst, st[:], reads=[b_st], writes=[b_pr])
    if PSTOP <= 1:
        return
    for cb in range(3):
        banks = [bank(S) for _ in range(4)]
        for jg in range(4):
            wt, b_wt, s_wt = S.w2.next()
            k.dma(k.sp, s_wt, wt[:], wtm_tile(cb, jg), reads=[b_wtm], writes=[b_wt])
            fns = []
            for jj in range(4):
                j = jg * 4 + jj
                for tb in range(4):
                    fns.append(lambda e, j=j, jj=jj, tb=tb: e.matmul(
                        banks[tb][0][:, :], lhsT=S.hT[:, j * TT + tb * 128: j * TT + (tb + 1) * 128],
                        rhs=wt[:, jj * 512:(jj + 1) * 512], start=(j == 0), stop=(j == KC - 1)))
            k.group(k.pe, fns, reads=[b_wt, S.b_hT], writes=[b for _, b in banks])
        for tb in range(4):
            vt, b_vt, s_vt = S.vt.items[tb]
            if tb % 2 == 0:
                k.op(k.act, lambda e: e.activation(out=vt[:, cb * 512:(cb + 1) * 512], in_=banks[tb][0][:, :], func=AF.Copy),
                     reads=[banks[tb][1]], writes=[b_vt])
            else:
                k.op(k.dve, lambda e: e.tensor_copy(out=vt[:, cb * 512:(cb + 1) * 512], in_=banks[tb][0][:, :]),
                     reads=[banks[tb][1]], writes=[b_vt])
    for tb in range(4):
        vt, b_vt, s_vt = S.vt.items[tb]
        k.dma(k.pool, s_vt, TMv[:, t0 + tb * 128: t0 + (tb + 1) * 128, :].rearrange("c p f -> p c f"),
              vt[:].rearrange("p (c f) -> p c f", c=3), reads=[b_vt], writes=[b_pr])
    if PSTOP <= 2:
        return
    pb, b_pb = bank(S)
    fns = [(lambda e, kc=kc: e.matmul(pb[:, :], lhsT=S.wab[:, kc * 128:(kc + 1) * 128],
                                      rhs=S.hT[:, kc * TT:(kc + 1) * TT], start=(kc == 0), stop=(kc == KC - 1)))
           for kc in range(KC)]
    k.group(k.pe, fns, reads=[S.b_wab, S.b_hT], writes=[b_pb])
    k.op(k.dve, lambda e: e.tensor_copy(out=S.abst[:], in_=pb[0:8, :]), reads=[b_pb], writes=[S.b_abst])
    k.dma(k.pool, S.s_abst, AB[:, t0:t0 + TT], S.abst[:], reads=[S.b_abst], writes=[b_pr])


def mixer_consts():
    c = {}
    p = np.arange(128)[:, None]
    f = np.arange(512)[None, :]
    am = np.zeros((20, 128, 512), np.float32)
    for m in range(20):
        d = f - p - 128 * (m - 16)
        cnt = ((d >= 0) & (d <= 128)).astype(np.float32)
        cnt += ((d >= 0) & (d <= 512) & (d % 4 == 0))
        cnt += ((d >= 0) & (d <= 2048) & (d % 16 == 0))
        am[m] = cnt
    c["amask"] = am.astype(ml_dtypes.bfloat16)
    bm = np.zeros((4, 128, 512), np.float32)
    for i in range(4):
        bm[i] = (p + 128 * i < f)
    c["bmask"] = bm.astype(ml_dtypes.bfloat16)
    j = np.arange(128)[:, None]
    s = np.arange(128)[None, :]
    c["negtri"] = (-(j >= s).astype(np.float32)).astype(ml_dtypes.bfloat16)
    c["ones"] = np.ones((128, 128), ml_dtypes.bfloat16)
    return c


class Pool2:
    def __init__(self, S, idxs):
        self.S, self.idxs, self.i = S, idxs, 0

    def next(self):
        i = self.idxs[self.i % len(self.idxs)]
        self.i += 1
        return self.S.ps[i], self.S.psb[i]


def setup_mix_common(S, cst, og_ap, S_len):
    k = S.k
    S.L = S_len
    S.ones_bf = k.sb("ones_bf", [128, 128], BF16)
    S.negtri = k.sb("negtri", [128, 128], BF16)
    S.one_col = k.sb("one_col", [128, 1], F32)
    S.ogP = k.sb("ogP", [128, 16], F32)
    S.b_mc = Buf("mixconst")
    sl = k.slot("mixconst")
    k.dma(k.sp, sl, S.ones_bf[:], cst["ones"], writes=[S.b_mc])
    k.dma(k.sp, sl, S.negtri[:], cst["negtri"], writes=[S.b_mc])
    k.dma(k.sp, sl, S.ogP[:].rearrange("p (g h) -> p g h", h=4), og_ap.rearrange("g (h p) -> p g h", p=128),
          writes=[S.b_mc], allow_slow_non_contiguous=True)
    k.op(k.dve, lambda e: e.memset(S.one_col[:], 1.0), writes=[S.b_mc])
    S.hp_sq = Ring([(k.sb(f"hp_sq{i}", [128, TT], BF16), Buf(f"hp_sq{i}")) for i in range(2)])
    S.hp_sd = Ring([(k.sb(f"hp_sd{i}", [128, TT], F32), Buf(f"hp_sd{i}")) for i in range(2)])
    S.hp_g = Ring([(k.sb(f"hp_g{i}", [128, TT], F32), Buf(f"hp_g{i}"), k.slot(f"hp_g{i}")) for i in range(2)])
    S.hp_o = Ring([(k.sb(f"hp_o{i}", [128, TT], BF16), Buf(f"hp_o{i}"), k.slot(f"hp_o{i}")) for i in range(2)])
    S.oT = Ring([(k.sb(f"oT{i}", [128, TT], F32), Buf(f"oT{i}")) for i in range(2)])


def head_post(S, oT, b_oT, hg, t0, n, mixedT, b_mixed, gate_ap=None, b_gate=None):
    k = S.k
    sq, b_sq = S.hp_sq.next()
    k.op(k.act, lambda e: e.activation(out=sq[:, 0:n], in_=oT, func=AF.Square), reads=[b_oT], writes=[b_sq])
    pb, b_pb = S.rot.next()
    k.op(k.pe, lambda e: e.matmul(pb[:, 0:n], lhsT=S.ones_bf[:], rhs=sq[:, 0:n], start=True, stop=True),
         reads=[b_sq, S.b_mc], writes=[b_pb])
    sd, b_sd = S.hp_sd.next()
    k.op(k.act, lambda e: e.activation(out=sd[:, 0:n], in_=pb[:, 0:n], func=AF.Sqrt, scale=1.0 / HD, bias=S.eps_col[:, 0:1]),
         reads=[b_pb, S.b_const], writes=[b_sd])
    k.op(k.dve, lambda e: e.reciprocal(out=sd[:, 0:n], in_=sd[:, 0:n]), reads=[b_sd], writes=[b_sd])
    ho, b_ho, s_ho = S.hp_o.next()
    if gate_ap is None:
        k.op(k.dve, lambda e: e.scalar_tensor_tensor(out=ho[:, 0:n], in0=oT, scalar=S.ogP[:, hg:hg + 1], in1=sd[:, 0:n],
                                                      op0=ALU.mult, op1=ALU.mult),
             reads=[b_oT, b_sd, S.b_mc], writes=[b_ho])
    else:
        gt, b_gt, s_gt = S.hp_g.next()
        k.dma(k.sp, s_gt, gt[:, 0:n], gate_ap, reads=[b_gate], writes=[b_gt])
        k.op(k.act, lambda e: e.activation(out=gt[:, 0:n], in_=gt[:, 0:n], func=AF.Silu), reads=[b_gt], writes=[b_gt])
        k.op(k.dve, lambda e: e.scalar_tensor_tensor(out=sd[:, 0:n], in0=oT, scalar=S.ogP[:, hg:hg + 1], in1=sd[:, 0:n],
                                                      op0=ALU.mult, op1=ALU.mult),
             reads=[b_oT, b_sd, S.b_mc], writes=[b_sd])
        k.op(k.pool, lambda e: e.tensor_tensor(out=ho[:, 0:n], in0=sd[:, 0:n], in1=gt[:, 0:n], op=ALU.mult),
             reads=[b_sd, b_gt], writes=[b_ho])
    return k.dma(k.pool, s_ho, mixedT[hg * 128:(hg + 1) * 128, t0:t0 + n], ho[:, 0:n], reads=[b_ho], writes=[b_mixed])


def setup_attn(S, cst):
    k = S.k
    L = S.L
    S.QT = k.sb("QT", [128, L], BF16)
    S.KT = k.sb("KT", [128, L], BF16)
    S.V = k.sb("Vtm", [128, L], BF16)
    S.b_qkv = Buf("qkv")
    S.s_qkv = k.slot("qkv")
    S.amask = k.sb("amask", [128, 20 * 512], BF16)
    S.bmask = k.sb("bmask", [128, 4 * 512], BF16)
    sl = k.slot("masks")
    k.dma(k.sp, sl, S.amask[:].rearrange("p (m f) -> p m f", f=512), cst["amask"].rearrange("m p f -> p m f"), writes=[S.b_mc])
    k.dma(k.sp, sl, S.bmask[:].rearrange("p (m f) -> p m f", f=512), cst["bmask"].rearrange("m p f -> p m f"), writes=[S.b_mc])
    S.E = Ring([(k.sb(f"E{i}", [128, TT], F32), Buf(f"E{i}")) for i in range(2)])
    S.Lb = Ring([(k.sb(f"Lb{i}", [128, TT], BF16), Buf(f"Lb{i}")) for i in range(2)])
    S.W = Ring([(k.sb(f"W{i}", [128, TT], F32), Buf(f"W{i}")) for i in range(2)])
    S.P = Ring([(k.sb(f"P{i}", [128, TT], BF16), Buf(f"P{i}")) for i in range(3)])
    S.carry = k.sb("carry", [128, TT], F32)
    S.b_carry = Buf("carry")


def load_head_qkv(S, FMb, iq, ik, TMv, iv, h, b_pr):
    k = S.k
    L = S.L
    k.dma(k.sp, S.s_qkv, S.QT[:], FMb[iq], reads=[b_pr], writes=[S.b_qkv])
    k.dma(k.sp, S.s_qkv, S.KT[:], FMb[ik], reads=[b_pr], writes=[S.b_qkv])
    for c0 in range(0, L, 1024):
        c1 = min(L, c0 + 1024)
        k.dma(k.sp, S.s_qkv, S.V[:, c0:c1].rearrange("p (n d) -> p n d", d=128),
              TMv[iv, c0:c1, h * 128:(h + 1) * 128].rearrange("(n p) d -> p n d", p=128), reads=[b_pr], writes=[S.b_qkv])


def mixer_A_head(S, hg, mixedT, b_mixed):
    k = S.k
    evs = []
    for qg in range(S.L // TT):
        t0 = qg * TT
        O, b_O = S.acc.next()
        Dn, b_Dn = S.acc.next()
        blocks = list(range(max(0, t0 - 2048), t0 + TT, 128))
        for idx, s0 in enumerate(blocks):
            m = (s0 - t0) // 128 + 16
            Z, b_Z = S.rot.next()
            k.op(k.pe, lambda e: e.matmul(Z[:, :], lhsT=S.KT[:, s0:s0 + 128], rhs=S.QT[:, t0:t0 + TT], start=True, stop=True),
                 reads=[S.b_qkv], writes=[b_Z])
            P, b_P = S.P.next()
            k.op(k.act, lambda e: e.activation(out=P[:], in_=Z[:, :], func=AF.Exp), reads=[b_Z], writes=[b_P])
            k.op(k.pool, lambda e: e.tensor_tensor(out=P[:], in0=P[:], in1=S.amask[:, m * 512:(m + 1) * 512], op=ALU.mult),
                 reads=[S.b_mc], writes=[b_P])
            k.group(k.pe, [
                lambda e: e.matmul(O[:, :], lhsT=S.V[:, s0:s0 + 128], rhs=P[:], start=(idx == 0), stop=(idx == len(blocks) - 1)),
                lambda e: e.matmul(Dn[:, :], lhsT=S.ones_bf[:], rhs=P[:], start=(idx == 0), stop=(idx == len(blocks) - 1)),
            ], reads=[b_P, S.b_qkv, S.b_mc], writes=[b_O, b_Dn])
        W, b_W = S.W.next()
        k.op(k.dve, lambda e: e.reciprocal(out=W[:], in_=Dn[:, :]), reads=[b_Dn], writes=[b_W])
        oT, b_oT = S.oT.next()
        k.op(k.dve, lambda e: e.tensor_tensor(out=oT[:], in0=O[:, :], in1=W[:], op=ALU.mult), reads=[b_O, b_W], writes=[b_oT])
        evs.append(head_post(S, oT[:], b_oT, hg, t0, TT, mixedT, b_mixed))
    return evs


def mixer_B_head(S, hg, mixedT, b_mixed):
    k = S.k
    evs = []
    for qg in range(S.L // TT):
        t0 = qg * TT
        O, b_O = S.acc.next()
        k.op(k.pool, lambda e: e.memset(S.carry[:], 0.0), writes=[S.b_carry])
        blocks = list(range(t0 + TT - 128, -1, -128))
        for idx, s0 in enumerate(blocks):
            diag = s0 >= t0
            mi = (s0 - t0) // 128
            Z, b_Z = S.rot.next()
            k.op(k.pe, lambda e: e.matmul(Z[:, :], lhsT=S.KT[:, s0:s0 + 128], rhs=S.QT[:, t0:t0 + TT], start=True, stop=True),
                 reads=[S.b_qkv], writes=[b_Z])
            E, b_E = S.E.next()
            k.op(k.act, lambda e: e.activation(out=E[:], in_=Z[:, :], func=AF.Exp), reads=[b_Z], writes=[b_E])
            Lb, b_Lb = S.Lb.next()
            k.op(k.act, lambda e: e.activation(out=Lb[:], in_=E[:], func=AF.Ln, bias=S.one_col[:, 0:1]),
                 reads=[b_E, S.b_mc], writes=[b_Lb])
            if diag:
                k.op(k.pool, lambda e: e.tensor_tensor(out=Lb[:], in0=Lb[:], in1=S.bmask[:, mi * 512:(mi + 1) * 512], op=ALU.mult),
                     reads=[S.b_mc], writes=[b_Lb])
            T, b_T = S.rot.next()
            C, b_C = S.rot.next()
            k.group(k.pe, [
                lambda e: e.matmul(T[:, :], lhsT=S.KT[:, s0:s0 + 128], rhs=S.QT[:, t0:t0 + TT], start=True, stop=False),
                lambda e: e.matmul(T[:, :], lhsT=S.negtri[:], rhs=Lb[:], start=False, stop=True),
                lambda e: e.matmul(C[:, :], lhsT=S.ones_bf[:], rhs=Lb[:], start=True, stop=True),
            ], reads=[b_Lb, S.b_qkv, S.b_mc], writes=[b_T, b_C])
            W, b_W = S.W.next()
            k.op(k.dve, lambda e: e.tensor_tensor(out=W[:], in0=T[:, :], in1=S.carry[:], op=ALU.subtract),
                 reads=[b_T, S.b_carry], writes=[b_W])
            k.op(k.dve, lambda e: e.tensor_tensor(out=S.carry[:], in0=C[:, :], in1=S.carry[:], op=ALU.add),
                 reads=[b_C], writes=[S.b_carry])
            P, b_P = S.P.next()
            k.op(k.act, lambda e: e.activation(out=P[:], in_=W[:], func=AF.Exp), reads=[b_W], writes=[b_P])
            if diag:
                k.op(k.pool, lambda e: e.tensor_tensor(out=P[:], in0=P[:], in1=S.bmask[:, mi * 512:(mi + 1) * 512], op=ALU.mult),
                     reads=[S.b_mc], writes=[b_P])
            k.op(k.pe, lambda e: e.matmul(O[:, :], lhsT=S.V[:, s0:s0 + 128], rhs=P[:], start=(idx == 0), stop=(idx == len(blocks) - 1)),
                 reads=[b_P, S.b_qkv], writes=[b_O])
        oT, b_oT = S.oT.next()
        k.op(k.act, lambda e: e.activation(out=oT[:], in_=O[:, :], func=AF.Copy), reads=[b_O], writes=[b_oT])
        evs.append(head_post(S, oT[:], b_oT, hg, t0, TT, mixedT, b_mixed))
    return evs


CC = 32
CD = 64


def mixer_cd_host(lb_logits, conv_w, a_log, dt_bias):
    d = {}
    d["lgP"] = np.ascontiguousarray(lb_logits.reshape(DEPTH, GH, 128).transpose(2, 0, 1).reshape(128, DEPTH * GH))
    d["cwP"] = np.ascontiguousarray(conv_w.reshape(4, 3, GH, 128).transpose(3, 1, 2, 0).reshape(128, 3 * GH * 4))
    d["alog"] = np.ascontiguousarray(a_log.reshape(1, GH))
    d["dtb"] = np.ascontiguousarray(dt_bias.reshape(1, GH))
    t = np.arange(TT)
    d["scan32"] = np.broadcast_to((t % CC != 0).astype(np.float32)[None, :], (128, TT)).copy()
    d["scan64"] = (t % CD != 0).astype(np.float32)[None, :].copy()
    p = np.arange(128)[:, None]
    d["mask32"] = ((p % CC) <= np.arange(CC)[None, :]).astype(np.float32).astype(ml_dtypes.bfloat16)
    s = np.arange(CD)[:, None]
    u = np.arange(CD)[None, :]
    d["maskSL"] = (u < s).astype(np.float32)
    d["maskUI"] = (s <= u).astype(np.float32)
    d["ident64"] = np.eye(CD, dtype=np.float32)
    return d


CD_SHAPES = {"lgP": ([128, DEPTH * GH], F32), "cwP": ([128, 3 * GH * 4], F32), "alog": ([1, GH], F32), "dtb": ([1, GH], F32),
             "scan32": ([128, TT], F32), "scan64": ([1, TT], F32), "mask32": ([128, CC], BF16),
             "maskSL": ([CD, CD], F32), "maskUI": ([CD, CD], F32), "ident64": ([CD, CD], F32)}


def mixer_cd_inputs(nc):
    return {n: nc.dram_tensor(n, sh, dt, kind="ExternalInput").ap() for n, (sh, dt) in CD_SHAPES.items()}


def mixer_C(S, FMf, TMv, ex, mixedT, b_pr, b_mixed, layer):
    k = S.k
    L = S.L
    S.acc = Pool2(S, [0, 1])
    S.rot = Pool2(S, [2, 3, 4, 5, 6, 7])
    sl = k.slot("c_const")
    b_cc = Buf("c_const")
    lgP = k.sb("lgP", [128, DEPTH * GH], F32)
    scan32 = k.sb("scan32", [128, TT], F32)
    mask32 = k.sb("mask32", [128, CC], BF16)
    k.dma(k.sp, sl, lgP[:], ex["lgP"], writes=[b_cc])
    k.dma(k.sp, sl, scan32[:], ex["scan32"], writes=[b_cc])
    k.dma(k.sp, sl, mask32[:], ex["mask32"], writes=[b_cc])
    lbt = k.sb("lbt", [128, GH], F32)
    oml = k.sb("oml", [128, GH], F32)
    ssum = k.sb("ssum", [128, GH], F32)
    k.op(k.act, lambda e: e.activation(out=lgP[:], in_=lgP[:], func=AF.Exp), reads=[b_cc], writes=[b_cc])
    k.op(k.dve, lambda e: e.tensor_tensor(out=ssum[:], in0=lgP[:, 0:GH], in1=lgP[:, GH:2 * GH], op=ALU.add), reads=[b_cc], writes=[b_cc])
    for l2 in range(2, DEPTH):
        k.op(k.dve, lambda e: e.tensor_tensor(out=ssum[:], in0=ssum[:], in1=lgP[:, l2 * GH:(l2 + 1) * GH], op=ALU.add), reads=[b_cc], writes=[b_cc])
    k.op(k.dve, lambda e: e.reciprocal(out=ssum[:], in_=ssum[:]), reads=[b_cc], writes=[b_cc])
    k.op(k.dve, lambda e: e.memset(lbt[:], 0.0), writes=[b_cc])
    for l2 in range(1, layer + 1):
        k.op(k.dve, lambda e: e.tensor_tensor(out=lbt[:], in0=lbt[:], in1=lgP[:, l2 * GH:(l2 + 1) * GH], op=ALU.add), reads=[b_cc], writes=[b_cc])
    k.op(k.dve, lambda e: e.tensor_tensor(out=lbt[:], in0=lbt[:], in1=ssum[:], op=ALU.mult), reads=[b_cc], writes=[b_cc])
    k.op(k.dve, lambda e: e.tensor_scalar(out=oml[:], in0=lbt[:], scalar1=-1.0, scalar2=1.0, op0=ALU.mult, op1=ALU.add),
         reads=[b_cc], writes=[b_cc])

    fr = Ring([(k.sb(f"c_fr{i}", [128, TT], F32), Buf(f"c_fr{i}"), k.slot(f"c_fr{i}")) for i in range(2)])
    qr = Ring([(k.sb(f"c_qr{i}", [128, TT], F32), Buf(f"c_qr{i}"), k.slot(f"c_qr{i}")) for i in range(2)])
    vr = Ring([(k.sb(f"c_v{i}", [64, 8 * 128], BF16), Buf(f"c_v{i}"), k.slot(f"c_v{i}")) for i in range(2)])
    bt = Ring([(k.sb(f"c_b{i}", [128, TT], F32), Buf(f"c_b{i}")) for i in range(2)])
    ebr = Ring([(k.sb(f"c_eb{i}", [128, TT], F32), Buf(f"c_eb{i}")) for i in range(2)])
    enb = Ring([(k.sb(f"c_enb{i}", [128, TT], F32), Buf(f"c_enb{i}")) for i in range(2)])
    kpT = Ring([(k.sb(f"c_kp{i}", [128, TT], BF16), Buf(f"c_kp{i}")) for i in range(2)])
    kppT = Ring([(k.sb(f"c_kpp{i}", [128, TT], BF16), Buf(f"c_kpp{i}")) for i in range(2)])
    qpT = Ring([(k.sb(f"c_qp{i}", [128, TT], BF16), Buf(f"c_qp{i}")) for i in range(2)])
    ktm = Ring([(k.sb(f"c_ktm{i}", [64, 8 * 128], BF16), Buf(f"c_ktm{i}")) for i in range(2)])
    stm = Ring([(k.sb(f"c_stm{i}", [128, CC], BF16), Buf(f"c_stm{i}")) for i in range(4)])
    Sf = k.sb("c_Sf", [128, 128], F32)
    b_Sf = Buf("c_Sf")
    Sb = Ring([(k.sb(f"c_Sb{i}", [128, 128], BF16), Buf(f"c_Sb{i}")) for i in range(2)])
    evs = []
    for h in range(GH):
        k.op(k.dve, lambda e: e.memset(Sf[:], 0.0), writes=[b_Sf])
        sb_cur, b_sb_cur = Sb.next()
        k.op(k.pool, lambda e: e.memset(sb_cur[:], 0.0), writes=[b_sb_cur])
        for g in range(L // TT):
            t0 = g * TT
            f_, b_f, s_f = fr.next()
            q_, b_q, s_q = qr.next()
            v_, b_v, s_v = vr.next()
            k.dma(k.sp, s_f, f_[:], FMf[4 + h, :, t0:t0 + TT], reads=[b_pr], writes=[b_f])
            k.dma(k.sp, s_q, q_[:], FMf[0 + h, :, t0:t0 + TT], reads=[b_pr], writes=[b_q])
            k.dma(k.sp, s_v, v_[:].rearrange("p (n d) -> p n d", d=128),
                  TMv[2, t0:t0 + TT, h * 128:(h + 1) * 128].rearrange("(n p) d -> p n d", p=64), reads=[b_pr], writes=[b_v])
            k.op(k.act, lambda e: e.activation(out=f_[:], in_=f_[:], func=AF.Sigmoid), reads=[b_f], writes=[b_f])
            k.op(k.dve, lambda e: e.tensor_scalar(out=f_[:], in0=f_[:], scalar1=oml[:, h:h + 1], scalar2=lbt[:, h:h + 1],
                                                  op0=ALU.mult, op1=ALU.add), reads=[b_f, b_cc], writes=[b_f])
            b_, b_b = bt.next()
            k.op(k.act, lambda e: e.activation(out=b_[:], in_=f_[:], func=AF.Ln), reads=[b_f], writes=[b_b])
            k.op(k.dve, lambda e: e.tensor_tensor_scan(out=b_[:], data0=scan32[:], data1=b_[:], initial=0.0,
                                                       op0=ALU.mult, op1=ALU.add), reads=[b_cc], writes=[b_b])
            eb, b_eb = ebr.next()
            en, b_en = enb.next()
            k.op(k.act, lambda e: e.activation(out=eb[:], in_=b_[:], func=AF.Exp), reads=[b_b], writes=[b_eb])
            k.op(k.act, lambda e: e.activation(out=en[:], in_=b_[:], func=AF.Exp, scale=-1.0), reads=[b_b], writes=[b_en])
            kp, b_kp = kpT.next()
            k.op(k.pool, lambda e: e.tensor_scalar(out=f_[:], in0=f_[:], scalar1=-1.0, scalar2=1.0, op0=ALU.mult, op1=ALU.add),
                 reads=[b_b], writes=[b_f])
            k.op(k.dve, lambda e: e.tensor_tensor(out=kp[:], in0=f_[:], in1=en[:], op=ALU.mult), reads=[b_f, b_en], writes=[b_kp])
            kpp, b_kpp = kppT.next()
            ebl = eb[:, CC - 1:TT:CC].unsqueeze(2).broadcast_to([128, TT // CC, CC])
            k.op(k.dve, lambda e: e.tensor_tensor(out=kpp[:].rearrange("p (c t) -> p c t", t=CC),
                                                  in0=kp[:].rearrange("p (c t) -> p c t", t=CC), in1=ebl, op=ALU.mult),
                 reads=[b_kp, b_eb], writes=[b_kpp])
            k.op(k.act, lambda e: e.activation(out=q_[:], in_=q_[:], func=AF.Silu), reads=[b_q], writes=[b_q])
            qp, b_qp = qpT.next()
            k.op(k.pool, lambda e: e.tensor_tensor(out=qp[:], in0=q_[:], in1=eb[:], op=ALU.mult), reads=[b_q, b_eb], writes=[b_qp])
            pt, b_pt = S.rot.next()
            ptb = pt[:].bitcast(BF16)
            k.group(k.pe, [(lambda e, n=n: e.transpose(out=ptb[0:64, n * 128:(n + 1) * 128], in_=kpp[:, n * 64:(n + 1) * 64],
                                                        identity=S.ident_b[:])) for n in range(8)],
                    reads=[b_kpp, S.b_const], writes=[b_pt])
            kt, b_kt = ktm.next()
            k.op(k.act, lambda e: e.activation(out=kt[:], in_=ptb[0:64, :], func=AF.Copy), reads=[b_pt], writes=[b_kt])
            O, b_O = S.acc.next()
            for c in range(TT // CC):
                n, r0 = c // 2, (c % 2) * CC
                cs = slice(c * CC, (c + 1) * CC)
                Z, b_Z = S.rot.next()
                k.op(k.pe, lambda e: e.matmul(Z[r0:r0 + CC, 0:CC], lhsT=kp[:, cs], rhs=qp[:, cs], start=True, stop=True),
                     reads=[b_kp, b_qp], writes=[b_Z])
                st, b_st = stm.next()
                k.op(k.dve, lambda e: e.tensor_tensor(out=st[r0:r0 + CC, :], in0=Z[r0:r0 + CC, 0:CC], in1=mask32[r0:r0 + CC, :],
                                                      op=ALU.mult), reads=[b_Z, b_cc], writes=[b_st])
                sb_prev, b_sb_prev = sb_cur, b_sb_cur
                k.group(k.pe, [
                    lambda e: e.matmul(O[:, cs], lhsT=sb_prev[:], rhs=qp[:, cs], start=True, stop=False),
                    lambda e: e.matmul(O[:, cs], lhsT=v_[r0:r0 + CC, n * 128:(n + 1) * 128], rhs=st[r0:r0 + CC, :], start=False, stop=True),
                ], reads=[b_sb_prev, b_qp, b_v, b_st], writes=[b_O])
                SU, b_SU = S.rot.next()
                k.op(k.pe, lambda e: e.matmul(SU[:, 0:128], lhsT=kt[r0:r0 + CC, n * 128:(n + 1) * 128],
                                              rhs=v_[r0:r0 + CC, n * 128:(n + 1) * 128], start=True, stop=True),
                     reads=[b_kt, b_v], writes=[b_SU])
                k.op(k.dve, lambda e: e.scalar_tensor_tensor(out=Sf[:], in0=Sf[:], scalar=eb[:, c * CC + CC - 1: c * CC + CC],
                                                              in1=SU[:, 0:128], op0=ALU.mult, op1=ALU.add),
                     reads=[b_SU, b_eb], writes=[b_Sf])
                sb_cur, b_sb_cur = Sb.next()
                k.op(k.act, lambda e: e.activation(out=sb_cur[:], in_=Sf[:], func=AF.Copy), reads=[b_Sf], writes=[b_sb_cur])
            oT, b_oT = S.oT.next()
            k.op(k.act, lambda e: e.activation(out=oT[:], in_=O[:, :], func=AF.Copy), reads=[b_O], writes=[b_oT])
            evs.append(head_post(S, oT[:], b_oT, 8 + h, t0, TT, mixedT, b_mixed, gate_ap=FMf[8 + h, :, t0:t0 + TT], b_gate=b_pr))
    return evs


def mixer_D(S, FMf, AB, ex, mixedT, b_pr, b_mixed):
    import os
    DSTOP = int(os.environ.get('DSTOP', '9'))
    DROWS = int(os.environ.get('DROWS', '99'))
    k = S.k
    L = S.L
    S.acc = Pool2(S, [0, 1])
    S.rot = Pool2(S, [2, 3, 4, 5, 6, 7])
    sl = k.slot("d_const")
    b_dc = Buf("d_const")
    cwP = k.sb("cwP", [128, 3 * GH * 4], F32)
    alog = k.sb("alog", [1, GH], F32)
    dtb = k.sb("dtb", [1, GH], F32)
    scan64 = k.sb("scan64", [1, TT], F32)
    maskSL = k.sb("maskSL", [CD, CD], F32)
    maskUI = k.sb("maskUI", [CD, CD], F32)
    id64 = k.sb("id64", [CD, CD], F32)
    ones_row = k.sb("ones_row", [1, 128], F32)
    for t_, n_ in [(cwP, "cwP"), (alog, "alog"), (dtb, "dtb"), (scan64, "scan64"), (maskSL, "maskSL"), (maskUI, "maskUI"), (id64, "ident64")]:
        k.dma(k.sp, sl, t_[:], ex[n_], writes=[b_dc])
    k.op(k.dve, lambda e: e.memset(ones_row[:], 1.0), writes=[b_dc])
    k.op(k.act, lambda e: e.activation(out=alog[:], in_=alog[:], func=AF.Exp), reads=[b_dc], writes=[b_dc])
    k.op(k.dve, lambda e: e.tensor_scalar(out=alog[:], in0=alog[:], scalar1=-1.0, scalar2=None, op0=ALU.mult), reads=[b_dc], writes=[b_dc])

    def ring(name, shape, dt, n=2, slot=False):
        if slot:
            return Ring([(k.sb(f"{name}{i}", shape, dt), Buf(f"{name}{i}"), k.slot(f"{name}{i}")) for i in range(n)])
        return Ring([(k.sb(f"{name}{i}", shape, dt), Buf(f"{name}{i}")) for i in range(n)])

    xin = [ring(f"d_x{a}", [128, TT + 3], F32, 2, True) for a in range(3)]
    cv = [ring(f"d_cv{a}", [128, TT], F32, 2) for a in range(3)]
    sqb = ring("d_sq", [128, TT], BF16, 2)
    rsd = ring("d_rs", [128, TT], F32, 2)
    qnb = ring("d_qnb", [128, TT], BF16, 2)
    knb = ring("d_knb", [128, TT], BF16, 2)
    qgb = ring("d_qgb", [128, TT], BF16, 2)
    Gb = ring("d_Gb", [128, TT], F32, 2)
    Eb = ring("d_Eb", [128, TT], F32, 2)
    ra = ring("d_ra", [1, TT], F32, 2, True)
    rb = ring("d_rb", [1, TT], F32, 2, True)
    rg = ring("d_rg", [1, TT], F32, 2)
    rbeg = ring("d_rbeg", [1, TT], F32, 2)
    red = ring("d_red", [1, TT], F32, 2)
    colS = ring("d_col", [CD, 4], F32, 3)
    rel = ring("d_rel", [CD, CD], F32, 3)
    relT = ring("d_relT", [CD, CD], F32, 3)
    Nm = ring("d_N", [CD, CD], F32, 8)
    TTr = ring("d_TT", [CD, CD], F32, 4)
    QKm = ring("d_QKm", [CD, CD], BF16, 3)
    ktok = ring("d_ktok", [CD, 128], F32, 3)
    kdb = ring("d_kd", [CD, 128], BF16, 3)
    bv = ring("d_bv", [CD, 128], F32, 3)
    ut = ring("d_u", [CD, 128], F32, 3)
    wTb = ring("d_wT", [128, CD], BF16, 3)
    vnew = ring("d_vnew", [CD, 128], BF16, 3)
    Sf = k.sb("d_Sf", [128, 128], F32)
    b_Sf = Buf("d_Sf")
    Sb = ring("d_Sb", [128, 128], BF16, 2)
    evs = []
    for h in range(GH):
        k.op(k.dve, lambda e: e.memset(Sf[:], 0.0), writes=[b_Sf])
        sb_cur, b_sb_cur = Sb.next()
        k.op(k.pool, lambda e: e.memset(sb_cur[:], 0.0), writes=[b_sb_cur])
        for g in range(L // TT):
            t0 = g * TT
            if DSTOP <= -1:
                continue
            cvs = []
            for a in range(3):
                x_, b_x, s_x = xin[a].next()
                if t0 == 0:
                    k.op(k.pool, lambda e: e.memset(x_[:, 0:3], 0.0), writes=[b_x])
                    k.dma(k.sp, s_x, x_[:, 3:TT + 3], FMf[12 + 4 * a + h, :, 0:TT], reads=[b_pr], writes=[b_x])
                else:
                    k.dma(k.sp, s_x, x_[:, :], FMf[12 + 4 * a + h, :, t0 - 3:t0 + TT], reads=[b_pr], writes=[b_x])
                c_, b_c = cv[a].next()
                wbase = (a * GH + h) * 4
                k.op(k.dve, lambda e: e.tensor_scalar(out=c_[:], in0=x_[:, 0:TT], scalar1=cwP[:, wbase:wbase + 1], scalar2=None, op0=ALU.mult),
                     reads=[b_x, b_dc], writes=[b_c])
                for tap in range(1, 4):
                    k.op(k.dve, lambda e, tap=tap: e.scalar_tensor_tensor(out=c_[:], in0=x_[:, tap:TT + tap], scalar=cwP[:, wbase + tap:wbase + tap + 1],
                                                                          in1=c_[:], op0=ALU.mult, op1=ALU.add), reads=[b_x, b_dc], writes=[b_c])
                k.op(k.act, lambda e: e.activation(out=c_[:], in_=c_[:], func=AF.Silu), reads=[], writes=[b_c])
                cvs.append((c_, b_c))
            nb = []
            for a in range(2):
                c_, b_c = cvs[a]
                sq, b_sq = sqb.next()
                k.op(k.act, lambda e: e.activation(out=sq[:], in_=c_[:], func=AF.Square), reads=[b_c], writes=[b_sq])
                pb, b_pb = S.rot.next()
                k.op(k.pe, lambda e: e.matmul(pb[:, :], lhsT=S.ones_bf[:], rhs=sq[:], start=True, stop=True), reads=[b_sq, S.b_mc], writes=[b_pb])
                rs, b_rs = rsd.next()
                k.op(k.act, lambda e: e.activation(out=rs[:], in_=pb[:, :], func=AF.Sqrt, bias=S.eps_col[:, 0:1]), reads=[b_pb, S.b_const], writes=[b_rs])
                k.op(k.dve, lambda e: e.reciprocal(out=rs[:], in_=rs[:]), reads=[], writes=[b_rs])
                k.op(k.dve, lambda e: e.scalar_tensor_tensor(out=c_[:], in0=c_[:], scalar=(QSCALE if a == 0 else 1.0), in1=rs[:],
                                                              op0=ALU.mult, op1=ALU.mult), reads=[b_rs], writes=[b_c])
                nbt, b_nbt = (qnb if a == 0 else knb).next()
                k.op(k.pool, lambda e: e.tensor_copy(out=nbt[:], in_=c_[:]), reads=[b_c], writes=[b_nbt])
                nb.append((nbt, b_nbt))
            (qn, b_qn), (kn, b_kn), (vc, b_vc) = cvs
            (qnbt, b_qnbt), (knbt, b_knbt) = nb
            if DSTOP <= 0:
                continue
            a_, b_a, s_a = ra.next()
            bb_, b_bb, s_bb = rb.next()
            if DROWS < 1:
                continue
            k.dma(k.sp, s_a, a_[:], AB[h:h + 1, t0:t0 + TT], reads=[b_pr], writes=[b_a])
            if DROWS < 2:
                continue
            k.dma(k.sp, s_bb, bb_[:], AB[4 + h:5 + h, t0:t0 + TT], reads=[b_pr], writes=[b_bb])
            if DROWS < 3:
                continue
            k.op(k.act, lambda e: e.activation(out=bb_[:], in_=bb_[:], func=AF.Sigmoid), reads=[], writes=[b_bb])
            if DROWS < 4:
                continue
            k.op(k.act, lambda e: e.activation(out=a_[:], in_=a_[:], func=AF.Exp, bias=dtb[0:1, h:h + 1]), reads=[b_dc], writes=[b_a])
            if DROWS < 5:
                continue
            k.op(k.act, lambda e: e.activation(out=a_[:], in_=a_[:], func=AF.Ln, bias=S.one_col[0:1, 0:1]), reads=[S.b_mc], writes=[b_a])
            if DROWS < 6:
                continue
            k.op(k.dve, lambda e: e.tensor_scalar(out=a_[:], in0=a_[:], scalar1=alog[0:1, h:h + 1], scalar2=None, op0=ALU.mult),
                 reads=[b_dc], writes=[b_a])
            g_, b_g = rg.next()
            if DROWS < 7:
                continue
            k.op(k.dve, lambda e: e.tensor_tensor_scan(out=g_[:], data0=scan64[:], data1=a_[:], initial=0.0, op0=ALU.mult, op1=ALU.add),
                 reads=[b_a, b_dc], writes=[b_g])
            ed_, b_ed = red.next()
            gl = g_[:, CD - 1:TT:CD].unsqueeze(2).broadcast_to([1, TT // CD, CD])
            if DROWS < 8:
                continue
            k.op(k.dve, lambda e: e.tensor_tensor(out=ed_[:].rearrange("p (c t) -> p c t", t=CD), in0=gl,
                                                  in1=g_[:].rearrange("p (c t) -> p c t", t=CD), op=ALU.subtract), reads=[b_g], writes=[b_ed])
            if DROWS < 9:
                continue
            k.op(k.act, lambda e: e.activation(out=ed_[:], in_=ed_[:], func=AF.Exp), reads=[], writes=[b_ed])
            beg, b_beg = rbeg.next()
            if DROWS < 10:
                continue
            k.op(k.act, lambda e: e.activation(out=beg[:], in_=g_[:], func=AF.Exp), reads=[b_g], writes=[b_beg])
            if DROWS < 11:
                continue
            k.op(k.dve, lambda e: e.tensor_tensor(out=beg[:], in0=beg[:], in1=bb_[:], op=ALU.mult), reads=[b_bb], writes=[b_beg])
            pg, b_pg = S.rot.next()
            if DROWS < 12:
                continue
            k.op(k.pe, lambda e: e.matmul(pg[:, :], lhsT=ones_row[:], rhs=g_[:], start=True, stop=True), reads=[b_g, b_dc], writes=[b_pg])
            Gb_, b_Gb = Gb.next()
            Eb_, b_Eb = Eb.next()
            if DROWS < 13:
                continue
            k.op(k.dve, lambda e: e.tensor_copy(out=Gb_[:], in_=pg[:, :]), reads=[b_pg], writes=[b_Gb])
            if DROWS < 14:
                continue
            k.op(k.act, lambda e: e.activation(out=Eb_[:], in_=pg[:, :], func=AF.Exp), reads=[b_pg], writes=[b_Eb])
            qg, b_qg = qgb.next()
            if DROWS < 15:
                continue
            k.op(k.pool, lambda e: e.tensor_tensor(out=qg[:], in0=qn[:], in1=Eb_[:], op=ALU.mult), reads=[b_qn, b_Eb], writes=[b_qg])
            O, b_O = S.acc.next()
            for c in range(TT // CD):
                if DSTOP <= 1:
                    continue
                cs = slice(c * CD, (c + 1) * CD)
                pc, b_pc = S.rot.next()
                k.group(k.pe, [
                    (lambda e, j=j, r=r: e.matmul(pc[0:CD, 2 * j:2 * j + 2], lhsT=r[0:1, cs], rhs=ones_row[0:1, 0:2], start=True, stop=True))
                    for j, r in enumerate([g_, bb_, beg, ed_])], reads=[b_g, b_bb, b_beg, b_ed, b_dc], writes=[b_pc])
                col, b_col = colS.next()
                k.op(k.dve, lambda e: e.tensor_copy(out=col[:], in_=pc[0:CD, 0:8:2]), reads=[b_pc], writes=[b_col])
                r1, b_r1 = rel.next()
                k.op(k.dve, lambda e: e.tensor_scalar(out=r1[:], in0=Gb_[0:CD, cs], scalar1=-1.0, scalar2=col[:, 0:1], op0=ALU.mult, op1=ALU.add),
                     reads=[b_Gb, b_col], writes=[b_r1])
                k.op(k.pool, lambda e: e.tensor_tensor(out=r1[:], in0=r1[:], in1=maskSL[:], op=ALU.mult), reads=[b_dc], writes=[b_r1])
                k.op(k.act, lambda e: e.activation(out=r1[:], in_=r1[:], func=AF.Exp), reads=[], writes=[b_r1])
                r2, b_r2 = relT.next()
                k.op(k.dve, lambda e: e.tensor_scalar(out=r2[:], in0=Gb_[0:CD, cs], scalar1=col[:, 0:1], scalar2=None, op0=ALU.subtract),
                     reads=[b_Gb, b_col], writes=[b_r2])
                k.op(k.pool, lambda e: e.tensor_tensor(out=r2[:], in0=r2[:], in1=maskUI[:], op=ALU.mult), reads=[b_dc], writes=[b_r2])
                k.op(k.act, lambda e: e.activation(out=r2[:], in_=r2[:], func=AF.Exp), reads=[], writes=[b_r2])
                pk, b_pk = S.rot.next()
                k.group(k.pe, [
                    lambda e: e.matmul(pk[0:CD, 0:CD], lhsT=knbt[:, cs], rhs=knbt[:, cs], start=True, stop=True),
                    lambda e: e.matmul(pk[0:CD, 64:64 + CD], lhsT=knbt[:, cs], rhs=qnbt[:, cs], start=True, stop=True),
                ], reads=[b_knbt, b_qnbt], writes=[b_pk])
                X, b_X = Nm.next()
                k.op(k.dve, lambda e: e.scalar_tensor_tensor(out=X[:], in0=pk[0:CD, 0:CD], scalar=col[:, 1:2], in1=r1[:], op0=ALU.mult, op1=ALU.mult),
                     reads=[b_pk, b_col, b_r1], writes=[b_X])
                k.op(k.pool, lambda e: e.tensor_tensor(out=X[:], in0=X[:], in1=maskSL[:], op=ALU.mult), reads=[b_dc], writes=[b_X])
                qk, b_qk = QKm.next()
                k.op(k.dve, lambda e: e.tensor_tensor(out=r2[:], in0=pk[0:CD, 64:64 + CD], in1=r2[:], op=ALU.mult), reads=[b_pk], writes=[b_r2])
                k.op(k.pool, lambda e: e.tensor_tensor(out=qk[:], in0=r2[:], in1=maskUI[:], op=ALU.mult), reads=[b_r2, b_dc], writes=[b_qk])
                if DSTOP <= 2:
                    continue
                pn, b_pn = S.rot.next()
                k.op(k.pe, lambda e: e.transpose(out=pn[0:CD, 0:CD], in_=X[:], identity=id64[:]), reads=[b_X, b_dc], writes=[b_pn])
                XT, b_XT = Nm.next()
                k.op(k.act, lambda e: e.activation(out=XT[:], in_=pn[0:CD, 0:CD], func=AF.Copy), reads=[b_pn], writes=[b_XT])
                Tt, b_Tt = TTr.next()
                k.op(k.dve, lambda e: e.tensor_tensor(out=Tt[:], in0=id64[:], in1=XT[:], op=ALU.subtract), reads=[b_XT, b_dc], writes=[b_Tt])
                for lev in range(5):
                    last = lev == 4
                    p2, b_p2 = S.rot.next()
                    fns = [lambda e: e.matmul(p2[0:CD, 0:CD], lhsT=XT[:], rhs=X[:], start=True, stop=True)]
                    if not last:
                        fns.append(lambda e: e.matmul(p2[0:CD, 64:64 + CD], lhsT=X[:], rhs=XT[:], start=True, stop=True))
                    k.group(k.pe, fns, reads=[b_X, b_XT], writes=[b_p2])
                    X2, b_X2 = Nm.next()
                    k.op(k.act, lambda e: e.activation(out=X2[:], in_=p2[0:CD, 0:CD], func=AF.Copy), reads=[b_p2], writes=[b_X2])
                    if not last:
                        X2T, b_X2T = Nm.next()
                        k.op(k.dve, lambda e: e.tensor_copy(out=X2T[:], in_=p2[0:CD, 64:64 + CD]), reads=[b_p2], writes=[b_X2T])
                    p3, b_p3 = S.rot.next()
                    k.op(k.pe, lambda e: e.matmul(p3[0:CD, 0:CD], lhsT=X2[:], rhs=Tt[:], start=True, stop=True), reads=[b_X2, b_Tt], writes=[b_p3])
                    Tn, b_Tn = TTr.next()
                    k.op(k.dve, lambda e: e.tensor_tensor(out=Tn[:], in0=p3[0:CD, 0:CD], in1=Tt[:], op=ALU.add), reads=[b_p3, b_Tt], writes=[b_Tn])
                    Tt, b_Tt = Tn, b_Tn
                    if not last:
                        X, b_X, XT, b_XT = X2, b_X2, X2T, b_X2T
                if DSTOP <= 3:
                    continue
                ptk, b_ptk = S.rot.next()
                k.group(k.pe, [
                    lambda e: e.transpose(out=ptk[0:CD, 0:128], in_=kn[:, cs], identity=S.ident_f[:]),
                    lambda e: e.transpose(out=ptk[0:CD, 128:256], in_=vc[:, cs], identity=S.ident_f[:]),
                ], reads=[b_kn, b_vc, S.b_const], writes=[b_ptk])
                kt_, b_kt = ktok.next()
                kd_, b_kd = kdb.next()
                bv_, b_bv = bv.next()
                k.op(k.dve, lambda e: e.tensor_scalar(out=kt_[:], in0=ptk[0:CD, 0:128], scalar1=col[:, 2:3], scalar2=None, op0=ALU.mult),
                     reads=[b_ptk, b_col], writes=[b_kt])
                k.op(k.act, lambda e: e.activation(out=kd_[:], in_=ptk[0:CD, 0:128], func=AF.Copy, scale=col[:, 3:4]),
                     reads=[b_ptk, b_col], writes=[b_kd])
                k.op(k.dve, lambda e: e.tensor_scalar(out=bv_[:], in0=ptk[0:CD, 128:256], scalar1=col[:, 1:2], scalar2=None, op0=ALU.mult),
                     reads=[b_ptk, b_col], writes=[b_bv])
                pu, b_pu = S.rot.next()
                k.group(k.pe, [
                    lambda e: e.matmul(pu[0:CD, 0:128], lhsT=Tt[:], rhs=bv_[:], start=True, stop=True),
                ], reads=[b_Tt, b_bv], writes=[b_pu])
                pw, b_pw = S.rot.next()
                k.op(k.pe, lambda e: e.matmul(pw[:, 0:CD], lhsT=kt_[:], rhs=Tt[:], start=True, stop=True), reads=[b_kt, b_Tt], writes=[b_pw])
                u_, b_u = ut.next()
                wT, b_wT = wTb.next()
                k.op(k.act, lambda e: e.activation(out=u_[:], in_=pu[0:CD, 0:128], func=AF.Copy), reads=[b_pu], writes=[b_u])
                k.op(k.dve, lambda e: e.tensor_copy(out=wT[:], in_=pw[:, 0:CD]), reads=[b_pw], writes=[b_wT])
                if DSTOP <= 4:
                    continue
                pws, b_pws = S.rot.next()
                k.op(k.pe, lambda e: e.matmul(pws[0:CD, 0:128], lhsT=wT[:], rhs=sb_cur[:], start=True, stop=True), reads=[b_wT, b_sb_cur], writes=[b_pws])
                vn, b_vn = vnew.next()
                k.op(k.dve, lambda e: e.tensor_tensor(out=vn[:], in0=u_[:], in1=pws[0:CD, 0:128], op=ALU.subtract), reads=[b_u, b_pws], writes=[b_vn])
                k.group(k.pe, [
                    lambda e: e.matmul(O[:, cs], lhsT=sb_cur[:], rhs=qg[:, cs], start=True, stop=False),
                    lambda e: e.matmul(O[:, cs], lhsT=vn[:], rhs=qk[:], start=False, stop=True),
                ], reads=[b_sb_cur, b_qg, b_vn, b_qk], writes=[b_O])
                psu, b_psu = S.rot.next()
                k.op(k.pe, lambda e: e.matmul(psu[:, 0:128], lhsT=kd_[:], rhs=vn[:], start=True, stop=True), reads=[b_kd, b_vn], writes=[b_psu])
                k.op(k.dve, lambda e: e.scalar_tensor_tensor(out=Sf[:], in0=Sf[:], scalar=Eb_[:, c * CD + CD - 1:c * CD + CD], in1=psu[:, 0:128],
                                                              op0=ALU.mult, op1=ALU.add), reads=[b_psu, b_Eb], writes=[b_Sf])
                sb_cur, b_sb_cur = Sb.next()
                k.op(k.act, lambda e: e.activation(out=sb_cur[:], in_=Sf[:], func=AF.Copy), reads=[b_Sf], writes=[b_sb_cur])
            if DSTOP <= 4:
                continue
            oT, b_oT = S.oT.next()
            k.op(k.act, lambda e: e.activation(out=oT[:], in_=O[:, :], func=AF.Copy), reads=[b_O], writes=[b_oT])
            evs.append(head_post(S, oT[:], b_oT, 12 + h, t0, TT, mixedT, b_mixed, gate_ap=FMf[24 + h, :, t0:t0 + TT], b_gate=b_pr))
    return evs


MODBLK = N_MOD * D_MODEL // 512


def build_program(S_len, depth):
    import os
    FSTOP = int(os.environ.get('FSTOP', '99'))
    nt = S_len // TT
    nc = bass.Bass("TRN2", target_bir_lowering=False)

    def din(name, shape, dt=F32):
        return nc.dram_tensor(name, shape, dt, kind="ExternalInput").ap()

    def dint(name, shape, dt):
        return nc.dram_tensor(name, shape, dt, kind="Internal").ap()

    x = din("x", [S_len, D_MODEL])
    cT = din("cT", [128, KC])
    wmod = din("wmod", [depth, MODBLK, 128, KC * 512])
    bmod = din("bmod", [depth, 1, N_MOD * D_MODEL])
    gain = din("gain", [depth, 6, D_MODEL])
    w13a = din("w13a", [depth, NFF, 128, KC * 256])
    w2a = din("w2a", [depth, 4, NFF // 4, 128, 2048])
    w13b = din("w13b", [depth, NFF, 128, KC * 256])
    w2b = din("w2b", [depth, 4, NFF // 4, 128, 2048])
    wfm = din("wfm", [depth, 22, 128, KC * 256])
    wtm = din("wtm", [depth, 3, 4, 128, 2048])
    wab = din("wab", [depth, 128, KC * 128])
    wout = din("wout", [depth, 4, 4, 128, 2048])
    og = din("og", [depth, 4, 512])
    ident = din("ident", [128, 128])
    mc = mixer_consts()
    cst = {n: din("c_" + n, list(v.shape), BF16) for n, v in mc.items()}
    cdc = {n: din(n, ([depth] + sh) if n in ("cwP", "alog", "dtb") else sh, dt) for n, (sh, dt) in CD_SHAPES.items()}
    out = nc.dram_tensor("out", [S_len, D_MODEL], F32, kind="ExternalOutput").ap()

    xs = dint("xs", [S_len, D_MODEL], F32)
    modD = dint("modD", [depth, N_MOD, D_MODEL], F32)
    FMb = dint("FMb", [NFMB, 128, S_len], BF16)
    FMf = dint("FMf", [NFMF, 128, S_len], F32)
    TMv = dint("TMv", [3, S_len, 512], BF16)
    AB = dint("AB", [8, S_len], F32)
    mixedT = dint("mixedT", [D_MODEL, S_len], BF16)
    s_w13a = dint("s_w13a", [NFF, 128, KC * 256], BF16)
    s_w2a = dint("s_w2a", [4, NFF // 4, 128, 2048], BF16)
    s_w13b = dint("s_w13b", [NFF, 128, KC * 256], BF16)
    s_w2b = dint("s_w2b", [4, NFF // 4, 128, 2048], BF16)
    s_wfm = dint("s_wfm", [22, 128, KC * 256], BF16)
    s_wtm = dint("s_wtm", [3, 4, 128, 2048], BF16)
    s_wout = dint("s_wout", [4, 4, 128, 2048], BF16)

    def flat(ap):
        nd = len(ap.shape)
        names = " ".join(f"d{i}" for i in range(nd))
        return ap.rearrange(f"{names} -> ({names})")

    with contextlib.ExitStack() as es:
        k = K(nc, es)
        S = setup_common(k, nc, es, {"ident": ident[:, :]})
        S.eps_col = k.sb("eps_col", [128, 1], F32)
        k.op(k.dve, lambda e: e.memset(S.eps_col[:], EPS), writes=[S.b_const])
        b_modD, b_xs, b_out, b_pr, b_mixed = Buf("modD"), Buf("xs"), Buf("out"), Buf("pr"), Buf("mixed")
        bw = {n: Buf(n) for n in ["w13a", "w2a", "w13b", "w2b", "wfm", "wtm", "wout"]}
        s_cast = k.slot("cast")
        s_fin = k.slot("fin")

        def fin():
            fe = [k.dma(k.sp, s_fin, out[t * TT:(t + 1) * TT, :], xs[t * TT:(t + 1) * TT, :], reads=[b_xs], writes=[b_out]) for t in range(nt)]
            k.finish(fe)
            return nc, mc

        with contextlib.ExitStack() as tes:
            k.enter(tes)
            ct = k.sb("ct", [128, KC], F32)
            b_ct = Buf("ct")
            s_ct = k.slot("ct")
            k.dma(k.sp, s_ct, ct[:], cT[:, :], writes=[b_ct])
            k.op(k.act, lambda e: e.activation(out=ct[:], in_=ct[:], func=AF.Silu), writes=[b_ct])
            wr = Ring([(k.sb(f"wm{i}", [128, KC * 512], F32), Buf(f"wm{i}"), k.slot(f"wm{i}")) for i in range(2)])
            br = Ring([(k.sb(f"bmr{i}", [1, 512], F32), Buf(f"bmr{i}"), k.slot(f"bmr{i}")) for i in range(2)])
            rr = Ring([(k.sb(f"mres{i}", [1, 512], F32), Buf(f"mres{i}"), k.slot(f"mres{i}")) for i in range(2)])
            modflat = modD.rearrange("l j d -> l (j d)")
            for l in range(depth):
                for blk in range(MODBLK):
                    wt, b_wt, s_wt = wr.next()
                    bt_, b_bt, s_bt = br.next()
                    rt, b_rt, s_rt = rr.next()
                    k.dma(k.sp, s_wt, wt[:], wmod[l, blk], writes=[b_wt])
                    k.dma(k.sp, s_bt, bt_[:], bmod[l, 0:1, blk * 512:(blk + 1) * 512], writes=[b_bt])
                    pb, b_pb = bank(S)
                    k.group(k.pe, [(lambda e, kc=kc: e.matmul(pb[0:1, :], lhsT=ct[:, kc:kc + 1], rhs=wt[:, kc * 512:(kc + 1) * 512],
                                                              start=(kc == 0), stop=(kc == KC - 1))) for kc in range(KC)],
                            reads=[b_ct, b_wt], writes=[b_pb])
                    k.op(k.dve, lambda e: e.tensor_tensor(out=rt[:], in0=pb[0:1, :], in1=bt_[:], op=ALU.add), reads=[b_pb, b_bt], writes=[b_rt])
                    k.dma(k.pool, s_rt, modflat[l:l + 1, blk * 512:(blk + 1) * 512], rt[:], reads=[b_rt], writes=[b_modD])
            k.reset()
            k.leave(es)
        if FSTOP <= 1:
            return nc, mc

        for l in range(depth):
            for nme, dst, src in [("w13a", s_w13a, w13a[l]), ("w2a", s_w2a, w2a[l]), ("wfm", s_wfm, wfm[l]), ("wtm", s_wtm, wtm[l]),
                                  ("wout", s_wout, wout[l]), ("w13b", s_w13b, w13b[l]), ("w2b", s_w2b, w2b[l])]:
                n_el = int(np.prod(dst.shape))
                cast_copy(k, s_cast, flat(dst), flat(src), n_el, [], [bw[nme]])

            def ffn_stage(first, w13s, b_w13, w2s, b_w2, n_pre, j0, n_post, src, b_src, dst, b_dst):
                set_stage_coefs(S, modD[l], gain[l], b_modD, n_pre=n_pre, j_sh=j0, j_sc=j0 + 1, j_g=j0 + 2, n_post=n_post, gscale=0.5)
                evs = []
                for t in range(nt):
                    t0 = t * TT
                    rows_in = lambda tb, t0=t0: src[t0 + tb * 128: t0 + (tb + 1) * 128, :]
                    rows_out = lambda tb, t0=t0: dst[t0 + tb * 128: t0 + (tb + 1) * 128, :]
                    prep_tile(S, rows_in, b_src)
                    ffn_tile(S, lambda j: w13s[j], b_w13, lambda cb, jg: w2s[cb, jg], b_w2)
                    evs += residual_epilogue(S, rows_in, b_src, rows_out, b_dst, store_q=(k.sp if dst is out else None))
                return evs

            with contextlib.ExitStack() as tes:
                k.enter(tes)
                setup_row(S, tes)
                setup_proj(S, tes)
                load_mod(S, modD[l], gain[l], b_modD)
                k.dma(k.pool, S.s_wab, S.wab[:], wab[l], writes=[S.b_wab])
                src, b_src = (x, Buf("xin")) if l == 0 else (xs, b_xs)
                ffn_stage(True, s_w13a, bw["w13a"], s_w2a, bw["w2a"], 0, 0, 1, src, b_src, xs, b_xs)
                k.reset()
                if FSTOP <= 2:
                    return fin()
                set_stage_coefs(S, modD[l], gain[l], b_modD, n_pre=2, j_sh=3, j_sc=4, j_g=None, n_post=None, gscale=1.0)
                for t in range(nt):
                    t0 = t * TT
                    prep_tile(S, lambda tb, t0=t0: xs[t0 + tb * 128: t0 + (tb + 1) * 128, :], b_xs)
                    proj_tile(S, t0, lambda pr: s_wfm[pr], bw["wfm"], lambda cb, jg: s_wtm[cb, jg], bw["wtm"], FMb, FMf, TMv, AB, b_pr)
                k.reset()
                k.leave(es)
            if FSTOP <= 3:
                return nc, mc

            with contextlib.ExitStack() as tes:
                k.enter(tes)
                setup_mix_common(S, cst, og[l], S_len)
                with contextlib.ExitStack() as tes2:
                    k.enter(tes2)
                    setup_attn(S, cst)
                    S.acc = Pool2(S, [0, 1, 2, 3])
                    S.rot = Pool2(S, [4, 5, 6, 7])
                    for h in range(GH):
                        load_head_qkv(S, FMb, h, 4 + h, TMv, 0, h, b_pr)
                        mixer_A_head(S, h, mixedT, b_mixed)
                    k.reset()
                    S.acc = Pool2(S, [0, 1])
                    S.rot = Pool2(S, [2, 3, 4, 5, 6, 7])
                    for h in range(GH):
                        load_head_qkv(S, FMb, 8 + h, 12 + h, TMv, 1, h, b_pr)
                        mixer_B_head(S, 4 + h, mixedT, b_mixed)
                        k.reset()
                    k.leave(tes)
                if FSTOP <= 4:
                    return nc, mc
                ex = {n: (cdc[n][l] if n in ("cwP", "alog", "dtb") else cdc[n]) for n in cdc}
                with contextlib.ExitStack() as tes2:
                    k.enter(tes2)
                    mixer_C(S, FMf, TMv, ex, mixedT, b_pr, b_mixed, l)
                    k.reset()
                    k.leave(tes)
                if FSTOP <= 5:
                    return nc, mc
                with contextlib.ExitStack() as tes2:
                    k.enter(tes2)
                    mixer_D(S, FMf, AB, ex, mixedT, b_pr, b_mixed)
                    k.reset()
                    k.leave(tes)
                k.leave(es)
            if FSTOP <= 6:
                fe = [k.dma(k.sp, s_fin, out[0:128, 0:S_len], FMf[4], reads=[b_pr], writes=[b_out]),
                      k.dma(k.pool, s_fin, out[128:256, 0:S_len], mixedT[1024:1152, :], reads=[b_mixed], writes=[b_out]),
                      k.dma(k.pool, s_fin, out[256:384, 0:S_len], mixedT[0:128, :], reads=[b_mixed], writes=[b_out])]
                k.finish(fe)
                return nc, mc

            with contextlib.ExitStack() as tes:
                k.enter(tes)
                setup_row(S, tes)
                load_mod(S, modD[l], gain[l], b_modD)
                set_stage_coefs(S, modD[l], gain[l], b_modD, n_pre=2, j_sh=3, j_sc=4, j_g=5, n_post=3, gscale=1.0)
                s_mx = k.slot("mx")
                b_hl = [S.b_hT] * KC
                for t in range(nt):
                    t0 = t * TT
                    k.dma(k.sp, s_mx, S.hT[:].rearrange("p (c t) -> p c t", t=TT),
                          mixedT[:, t0:t0 + TT].rearrange("(c p) t -> p c t", p=128), reads=[b_mixed], writes=[S.b_hT])
                    mm_tokmajor(S, lambda j: S.hT[:, j * TT:(j + 1) * TT], b_hl, KC, lambda cb, jg: s_wout[cb, jg], bw["wout"],
                                lambda tb, cb: S.y[:, tb * D_MODEL + cb * 512: tb * D_MODEL + (cb + 1) * 512], S.b_y)
                    rows = lambda tb, t0=t0: xs[t0 + tb * 128: t0 + (tb + 1) * 128, :]
                    residual_epilogue(S, rows, b_xs, rows, b_xs)
                k.reset()
                if FSTOP <= 7:
                    return fin()
                last = l == depth - 1
                dst, b_dst = (xs, b_xs)
                evs = ffn_stage(False, s_w13b, bw["w13b"], s_w2b, bw["w2b"], 4, 6, 5, xs, b_xs, dst, b_dst)
                k.reset()
                if last:
                    fin()
                k.leave(es)
        print("program: ninst", k.ninst, "ndma", k.ndma, "nsem", k.nsem, flush=True)
    return nc, mc


def host_inputs(depth, c_b, inputs, mc):
    f = {}
    wm = inputs["w_mod"][:depth]
    f["wmod"] = np.ascontiguousarray(wm.reshape(depth, KC, 128, MODBLK, 512).transpose(0, 3, 2, 1, 4).reshape(depth, MODBLK, 128, KC * 512))
    f["bmod"] = np.ascontiguousarray(inputs["b_mod"][:depth].reshape(depth, 1, N_MOD * D_MODEL))
    f["gain"] = np.ascontiguousarray(inputs["norm_gain"][:depth])
    f["w13a"] = np.stack([lay_w13(inputs["ffn1_w13"][l]) for l in range(depth)])
    f["w2a"] = np.stack([lay_w2(inputs["ffn1_w2"][l], NFF) for l in range(depth)])
    f["w13b"] = np.stack([lay_w13(inputs["ffn2_w13"][l]) for l in range(depth)])
    f["w2b"] = np.stack([lay_w2(inputs["ffn2_w2"][l], NFF) for l in range(depth)])
    wi = [lay_win(inputs["w_in"][l]) for l in range(depth)]
    f["wfm"] = np.stack([w[0] for w in wi])
    f["wtm"] = np.stack([w[1] for w in wi])
    f["wab"] = np.stack([w[2] for w in wi])
    f["wout"] = np.stack([lay_w2(inputs["w_out"][l], KC) for l in range(depth)])
    f["og"] = np.ascontiguousarray(inputs["mix_out_gain"][:depth])
    f["ident"] = np.eye(128, dtype=np.float32)
    for n, v in mc.items():
        f["c_" + n] = v
    cds = [mixer_cd_host(inputs["hgrn_lb_logits"], inputs["dn_conv_w"][l], inputs["dn_a_log"][l], inputs["dn_dt_bias"][l]) for l in range(depth)]
    for n in CD_SHAPES:
        if n in ("cwP", "alog", "dtb"):
            f[n] = np.stack([cd[n] for cd in cds])
        else:
            f[n] = cds[0][n]
    return f


def kernel(**inputs):
    inputs = {k_: np.asarray(v) for k_, v in inputs.items()}
    nc, mc = build_program(SEQ, DEPTH)
    shared = host_inputs(DEPTH, None, inputs, mc)
    in_maps = []
    for b in range(BATCH):
        m = dict(shared)
        m["x"] = np.ascontiguousarray(inputs["x"][b])
        m["cT"] = np.ascontiguousarray(inputs["c"][b].reshape(KC, 128).T)
        in_maps.append(m)
    res = run_bass_kernel_spmd(nc, in_maps, core_ids=list(range(BATCH)))
    return np.stack([r["out"] for r in res.results], axis=0).astype(np.float32)
```

```python
import contextlib
import numpy as np
import ml_dtypes
import concourse.bass as bass
import concourse.mybir as mybir
from concourse.bass_utils import run_bass_kernel_spmd
from concourse.alu_op_type import AluOpType as ALU

F32 = mybir.dt.float32
BF16 = mybir.dt.bfloat16
I32 = mybir.dt.int32
AF = mybir.ActivationFunctionType
AX = mybir.AxisListType

D_MODEL = 2048
BATCH = 4
SEQ = 8192
DEPTH = 4
HD = 128
GW = 512
GH = 4
D_FF = 5632
NFF = D_FF // 128
KC = D_MODEL // 128
IN_COLS = 14 * GW + 2 * GH
N_MOD = 9
EPS = 1e-6
NCORES = 8
TOK_CORE = BATCH * SEQ // NCORES
TT = 512
NT = TOK_CORE // TT


class Buf:
    __slots__ = ("name", "w", "r", "excl")

    def __init__(self, name, excl=False):
        self.name = name
        self.excl = excl
        self.w = None
        self.r = []


class Eng:
    def __init__(self, k, name, handle, sem, selfsync=True):
        self.k = k
        self.name = name
        self.h = handle
        self.sem = sem
        self.count = 0
        self.waited = {}
        self.selfsync = selfsync

    def wait(self, ev):
        if ev is None:
            return
        sem, val, owner, epoch = ev
        if epoch != self.k.epoch:
            return
        if owner is self and not self.selfsync:
            return
        key = id(sem)
        if self.waited.get(key, 0) >= val:
            return
        self.waited[key] = val
        self.h.wait_ge(sem, val)


class K:
    def __init__(self, nc, es):
        self.nc = nc
        self.es = es
        self.nsem = 0
        self.epoch = 0
        self.tes = es
        self.free_slots = []
        self.live_slots = []
        self.marks = []
        self.bar1 = self.newsem("bar1")
        self.bar2 = self.newsem("bar2")
        self.nbar = 0
        self.pe = Eng(self, "pe", nc.tensor, self.newsem("pe"), selfsync=False)
        self.act = Eng(self, "act", nc.scalar, self.newsem("act"))
        self.dve = Eng(self, "dve", nc.vector, self.newsem("dve"))
        self.pool = Eng(self, "pool", nc.gpsimd, self.newsem("pool"))
        self.sp = Eng(self, "sp", nc.sync, self.newsem("sp"))
        self.engs = [self.pe, self.act, self.dve, self.pool, self.sp]
        self.dma_sems = []
        self.ndma = 0
        self.ninst = 0

    def newsem(self, name):
        self.nsem += 1
        return self.es.enter_context(self.nc.semaphore(f"{name}_{self.nsem}"))

    def sb(self, name, shape, dt):
        self.nsb = getattr(self, "nsb", 0) + 1
        return self.tes.enter_context(self.nc.sbuf_tensor(f"sb{self.nsb}_{name}", shape, dt))

    def _deps(self, eng, reads, writes):
        for b in reads:
            eng.wait(b.w)
            if b.excl:
                for ev in b.r:
                    eng.wait(ev)
        for b in writes:
            eng.wait(b.w)
            for ev in b.r:
                eng.wait(ev)

    def _commit(self, ev, reads, writes):
        for b in reads:
            b.r.append(ev)
            if len(b.r) > 24:
                b.r = b.r[-24:]
        for b in writes:
            b.w = ev
            b.r = []

    def op(self, eng, fn, reads=(), writes=()):
        self._deps(eng, reads, writes)
        ins = fn(eng.h)
        eng.count += 1
        ins.then_inc(eng.sem, 1)
        ev = (eng.sem, eng.count, eng, self.epoch)
        self._commit(ev, reads, writes)
        self.ninst += 1
        return ev

    def group(self, eng, fns, reads=(), writes=()):
        self._deps(eng, reads, writes)
        ins = None
        for fn in fns:
            ins = fn(eng.h)
            self.ninst += 1
        eng.count += 1
        ins.then_inc(eng.sem, 1)
        ev = (eng.sem, eng.count, eng, self.epoch)
        self._commit(ev, reads, writes)
        return ev

    def dma(self, q, slot, out, in_, reads=(), writes=(), **kw):
        self._deps(q, reads, writes)
        ins = q.h.dma_start(out=out, in_=in_, **kw)
        slot.count += 16
        ins.then_inc(slot.sem, 16)
        ev = (slot.sem, slot.count, None, self.epoch)
        self._commit(ev, reads, writes)
        self.ndma += 1
        return ev

    def slot(self, name):
        if self.free_slots:
            s = self.free_slots.pop()
        else:
            s = Slot(self.newsem(name))
            self.dma_sems.append(s)
        self.live_slots.append(s)
        return s

    def enter(self, tes):
        self.tes = tes
        self.marks.append(len(self.live_slots))

    def leave(self, prev):
        m = self.marks.pop()
        self.free_slots.extend(self.live_slots[m:])
        del self.live_slots[m:]
        self.tes = prev

    def barrier(self):
        evs = [(e.sem, e.count, None, self.epoch) for e in self.engs if e.count > 0]
        evs += [(s.sem, s.count, None, self.epoch) for s in self.dma_sems if s.count > 0]
        for e in self.engs:
            for ev in evs:
                if ev[0] is e.sem:
                    continue
                e.wait(ev)

    def reset(self):
        self.barrier()
        return
        self.nbar += 1
        n = len(self.engs) * self.nbar
        for e in self.engs:
            e.h.sem_inc(self.bar1, 1)
        for e in self.engs:
            e.h.wait_ge(self.bar1, n)
            e.h.sem_clear(e.sem)
            if e is self.sp:
                for s in self.dma_sems:
                    e.h.sem_clear(s.sem)
            e.h.sem_inc(self.bar2, 1)
        for e in self.engs:
            e.h.wait_ge(self.bar2, n)
            e.count = 0
            e.waited = {}
        for s in self.dma_sems:
            s.count = 0
        self.epoch += 1

    def finish(self, evs):
        for ev in evs:
            self.sp.wait(ev)


class Slot:
    def __init__(self, sem):
        self.sem = sem
        self.count = 0


class Ring:
    def __init__(self, items):
        self.items = items
        self.i = 0

    def next(self):
        it = self.items[self.i % len(self.items)]
        self.i += 1
        return it


class Ctx:
    pass


def make_consts():
    c = {}
    ident = np.eye(128, dtype=np.float32)
    c["ident"] = ident
    return c


def setup_common(k, nc, es, consts_ap):
    S = Ctx()
    S.k, S.nc = k, nc
    S.ps = [es.enter_context(nc.psum_tensor(f"psb{i}", [128, 512], F32)) for i in range(8)]
    S.psb = [Buf(f"psb{i}", excl=True) for i in range(8)]
    S.psi = 0
    S.ident_f = k.sb("ident_f", [128, 128], F32)
    S.ident_b = k.sb("ident_b", [128, 128], BF16)
    S.b_const = Buf("const")
    sl = k.slot("const")
    k.dma(k.sp, sl, S.ident_f[:], consts_ap["ident"], writes=[S.b_const])
    k.op(k.dve, lambda e: e.tensor_copy(out=S.ident_b[:], in_=S.ident_f[:]), reads=[S.b_const], writes=[S.b_const])
    return S


def bank(S):
    i = S.psi % 8
    S.psi += 1
    return S.ps[i], S.psb[i]


def rstd_from_ss(S, ss, b_ss, n, tag):
    k = S.k
    k.op(k.act, lambda e: e.activation(out=ss, in_=ss, func=AF.Sqrt, scale=1.0 / n, bias=S.eps_col[:, 0:1]),
         reads=[b_ss, S.b_const], writes=[b_ss])
    k.op(k.dve, lambda e: e.reciprocal(out=ss, in_=ss), reads=[b_ss], writes=[b_ss])


def setup_row(S, es):
    k = S.k
    S.hT = k.sb("hT", [128, KC * TT], BF16)
    S.b_hT = Buf("hT")
    S.GT = k.sb("GT", [128, NFF * TT], BF16)
    S.b_GT = [Buf(f"GT{j}") for j in range(NFF)]
    S.w13 = Ring([(k.sb(f"w13_{i}", [128, KC * 256], BF16), Buf(f"w13_{i}"), k.slot(f"w13_{i}")) for i in range(3)])
    S.w2 = Ring([(k.sb(f"w2_{i}", [128, 4 * 512], BF16), Buf(f"w2_{i}"), k.slot(f"w2_{i}")) for i in range(3)])
    S.y = k.sb("y", [128, 4 * D_MODEL], F32)
    S.b_y = [Buf(f"y{i}") for i in range(4)]
    S.xr = Ring([(k.sb(f"x_{i}", [128, D_MODEL], F32), Buf(f"x_{i}"), k.slot(f"x_{i}")) for i in range(2)])
    S.xn = Ring([(k.sb(f"xn_{i}", [128, D_MODEL], BF16), Buf(f"xn_{i}")) for i in range(2)])
    S.junk = k.sb("junk", [128, D_MODEL], BF16)
    S.b_junk = Buf("junk")
    S.sg = Ring([(k.sb(f"sg_{i}", [128, TT], F32), Buf(f"sg_{i}")) for i in range(2)])
    S.grow = k.sb("grow", [128, D_MODEL], F32)
    S.b_grow = Buf("grow")
    S.tmp = k.sb("tmp", [128, D_MODEL], F32)
    S.b_tmp = Buf("tmp")
    S.ss = Ring([(k.sb(f"ss_{i}", [128, 1], F32), Buf(f"ss_{i}")) for i in range(4)])
    S.modP = k.sb("modP", [128, N_MOD * KC], F32)
    S.gainP = k.sb("gainP", [128, 6 * KC], F32)
    S.coefA = k.sb("coefA", [128, KC], F32)
    S.b_mod = Buf("mod")
    S.b_coef = Buf("coef")
    S.s_mod = k.slot("mod")
    S.s_grow = k.slot("grow")
    S.eps_col = k.sb("eps_col", [128, 1], F32)
    k.op(k.dve, lambda e: e.memset(S.eps_col[:], EPS), writes=[S.b_const])


def load_mod(S, mod_ap, gain_ap, b_moddram):
    k = S.k
    k.dma(k.sp, S.s_mod, S.modP[:].rearrange("p (j c) -> p j c", c=KC),
          mod_ap.rearrange("j (c p) -> p j c", p=128), reads=[b_moddram], writes=[S.b_mod],
          allow_slow_non_contiguous=True)
    k.dma(k.sp, S.s_mod, S.gainP[:].rearrange("p (j c) -> p j c", c=KC),
          gain_ap.rearrange("j (c p) -> p j c", p=128), reads=[b_moddram], writes=[S.b_mod],
          allow_slow_non_contiguous=True)


def set_stage_coefs(S, mod_ap, gain_ap, b_moddram, n_pre, j_sh, j_sc, j_g, n_post, gscale):
    k = S.k
    k.op(k.dve, lambda e: e.scalar_tensor_tensor(
        out=S.coefA[:], in0=S.modP[:, j_sc * KC:(j_sc + 1) * KC], scalar=1.0,
        in1=S.gainP[:, n_pre * KC:(n_pre + 1) * KC], op0=ALU.add, op1=ALU.mult),
        reads=[S.b_mod], writes=[S.b_coef])
    S.shift = S.modP[:, j_sh * KC:(j_sh + 1) * KC]
    if j_g is not None:
        k.dma(k.sp, S.s_grow, S.grow[:], mod_ap[j_g:j_g + 1, :].partition_broadcast(128),
              reads=[b_moddram], writes=[S.b_grow])
        k.dma(k.sp, S.s_grow, S.tmp[:], gain_ap[n_post:n_post + 1, :].partition_broadcast(128),
              reads=[b_moddram], writes=[S.b_tmp])
        k.op(k.dve, lambda e: e.scalar_tensor_tensor(
            out=S.grow[:], in0=S.grow[:], scalar=float(gscale), in1=S.tmp[:], op0=ALU.mult, op1=ALU.mult),
            reads=[S.b_grow, S.b_tmp], writes=[S.b_grow])


def _pick(b, tb):
    return b[tb] if isinstance(b, list) else b


def prep_tile(S, x_rows, b_x):
    k = S.k
    for tb in range(TT // 128):
        xt, b_xt, s_xt = S.xr.next()
        k.dma(k.sp, s_xt, xt[:], x_rows(tb), reads=[_pick(b_x, tb)], writes=[b_xt])
        ss, b_ss = S.ss.next()
        k.op(k.act, lambda e: e.activation(out=S.junk[:], in_=xt[:], func=AF.Square, accum_out=ss[:]),
             reads=[b_xt], writes=[S.b_junk, b_ss])
        rstd_from_ss(S, ss[:], b_ss, D_MODEL, "p")
        xn, b_xn = S.xn.next()
        k.op(k.act, lambda e: e.activation(out=xn[:], in_=xt[:], func=AF.Copy, scale=ss[:, 0:1]),
             reads=[b_xt, b_ss], writes=[b_xn])
        for half in range(2):
            pt, b_pt = bank(S)
            ptb = pt[:].bitcast(BF16)
            fns = []
            for i in range(8):
                fc = half * 8 + i
                fns.append(lambda e, i=i, fc=fc: e.transpose(
                    out=ptb[:, i * 128:(i + 1) * 128], in_=xn[:, fc * 128:(fc + 1) * 128], identity=S.ident_b[:]))
            k.group(k.pe, fns, reads=[b_xn, S.b_const], writes=[b_pt])
            for i in range(8):
                fc = half * 8 + i
                dst = S.hT[:, fc * TT + tb * 128: fc * TT + (tb + 1) * 128]
                src = ptb[:, i * 128:(i + 1) * 128]
                if half == 0:
                    k.op(k.dve, lambda e, dst=dst, src=src, fc=fc: e.tensor_scalar(
                        out=dst, in0=src, scalar1=S.coefA[:, fc:fc + 1], scalar2=S.shift[:, fc:fc + 1],
                        op0=ALU.mult, op1=ALU.add), reads=[b_pt, S.b_coef, S.b_mod], writes=[S.b_hT])
                else:
                    k.op(k.act, lambda e, dst=dst, src=src, fc=fc: e.activation(
                        out=dst, in_=src, func=AF.Identity, scale=S.coefA[:, fc:fc + 1], bias=S.shift[:, fc:fc + 1]),
                        reads=[b_pt, S.b_coef, S.b_mod], writes=[S.b_hT])


def mm_tokmajor(S, lhs, b_lhs, nk, w_tile, b_w, y_dst, b_y):
    k = S.k
    ng = nk // 4
    for cb in range(4):
        banks = [bank(S) for _ in range(4)]
        for jg in range(ng):
            wt, b_wt, s_wt = S.w2.next()
            k.dma(k.sp, s_wt, wt[:], w_tile(cb, jg), reads=[b_w], writes=[b_wt])
            fns = []
            for jj in range(4):
                j = jg * 4 + jj
                for tb in range(4):
                    fns.append(lambda e, j=j, jj=jj, tb=tb: e.matmul(
                        banks[tb][0][:, :], lhsT=lhs(j)[:, tb * 128:(tb + 1) * 128],
                        rhs=wt[:, jj * 512:(jj + 1) * 512], start=(j == 0), stop=(j == nk - 1)))
            k.group(k.pe, fns, reads=[b_wt] + [b_lhs[jg * 4 + jj] for jj in range(4)],
                    writes=[b for _, b in banks])
        for tb in range(4):
            dst = y_dst(tb, cb)
            if tb % 2 == 0:
                k.op(k.act, lambda e, dst=dst, tb=tb: e.activation(out=dst, in_=banks[tb][0][:, :], func=AF.Copy),
                     reads=[banks[tb][1]], writes=[b_y[tb]])
            else:
                k.op(k.dve, lambda e, dst=dst, tb=tb: e.tensor_copy(out=dst, in_=banks[tb][0][:, :]),
                     reads=[banks[tb][1]], writes=[b_y[tb]])


def residual_epilogue(S, x_rows_in, b_xin, x_rows_out, b_xout, store_q=None):
    k = S.k
    evs = []
    for tb in range(4):
        ysl = S.y[:, tb * D_MODEL:(tb + 1) * D_MODEL]
        ss, b_ss = S.ss.next()
        k.op(k.act, lambda e: e.activation(out=S.junk[:], in_=ysl, func=AF.Square, accum_out=ss[:]),
             reads=[S.b_y[tb]], writes=[S.b_junk, b_ss])
        rstd_from_ss(S, ss[:], b_ss, D_MODEL, "e")
        xt, b_xt, s_xt = S.xr.next()
        k.dma(k.sp, s_xt, xt[:], x_rows_in(tb), reads=[_pick(b_xin, tb)], writes=[b_xt])
        k.op(k.dve, lambda e: e.scalar_tensor_tensor(
            out=S.tmp[:], in0=ysl, scalar=ss[:, 0:1], in1=S.grow[:], op0=ALU.mult, op1=ALU.mult),
            reads=[S.b_y[tb], b_ss, S.b_grow], writes=[S.b_tmp])
        k.op(k.pool, lambda e: e.tensor_tensor(out=xt[:], in0=S.tmp[:], in1=xt[:], op=ALU.add),
             reads=[S.b_tmp], writes=[b_xt])
        evs.append(k.dma(store_q or k.pool, s_xt, x_rows_out(tb), xt[:], reads=[b_xt], writes=[_pick(b_xout, tb)]))
    return evs


def ffn_tile(S, w13_tile, b_w13, w2_tile, b_w2):
    k = S.k
    for j in range(NFF):
        wt, b_wt, s_wt = S.w13.next()
        k.dma(k.sp, s_wt, wt[:], w13_tile(j), reads=[b_w13], writes=[b_wt])
        pg, b_pg = bank(S)
        pu, b_pu = bank(S)
        fns = []
        for kc in range(KC):
            fns.append(lambda e, kc=kc: e.matmul(pg[:, :], lhsT=wt[:, kc * 256: kc * 256 + 128],
                                                  rhs=S.hT[:, kc * TT:(kc + 1) * TT], start=(kc == 0), stop=(kc == KC - 1)))
        for kc in range(KC):
            fns.append(lambda e, kc=kc: e.matmul(pu[:, :], lhsT=wt[:, kc * 256 + 128: kc * 256 + 256],
                                                  rhs=S.hT[:, kc * TT:(kc + 1) * TT], start=(kc == 0), stop=(kc == KC - 1)))
        k.group(k.pe, fns, reads=[b_wt, S.b_hT], writes=[b_pg, b_pu])
        sg, b_sg = S.sg.next()
        k.op(k.act, lambda e: e.activation(out=sg[:], in_=pg[:, :], func=AF.Silu), reads=[b_pg], writes=[b_sg])
        k.op(k.dve, lambda e, j=j: e.tensor_tensor(out=S.GT[:, j * TT:(j + 1) * TT], in0=sg[:], in1=pu[:, :], op=ALU.mult),
             reads=[b_sg, b_pu], writes=[S.b_GT[j]])
    mm_tokmajor(S, lambda j: S.GT[:, j * TT:(j + 1) * TT], S.b_GT, NFF, w2_tile, b_w2,
                lambda tb, cb: S.y[:, tb * D_MODEL + cb * 512: tb * D_MODEL + (cb + 1) * 512], S.b_y)


def cast_copy(k, slot, dst_flat, src_flat, n, reads, writes):
    assert n % 2048 == 0
    rows = n // 2048
    d2 = dst_flat.rearrange("(r c) -> r c", c=2048)
    s2 = src_flat.rearrange("(r c) -> r c", c=2048)
    ev = None
    r0 = 0
    while r0 < rows:
        r1 = min(rows, r0 + 4096)
        ev = k.dma(k.pool, slot, d2[r0:r1, :], s2[r0:r1, :], reads=reads, writes=writes)
        r0 = r1
    return ev


def lay_w13(w13):
    g = w13[:, :D_FF].reshape(KC, 128, NFF, 128)
    u = w13[:, D_FF:].reshape(KC, 128, NFF, 128)
    t = np.stack([g, u], axis=3)
    return np.ascontiguousarray(t.transpose(2, 1, 0, 3, 4).reshape(NFF, 128, KC * 256))


def lay_w2(w2, nk):
    t = w2.reshape(nk // 4, 4, 128, 4, 512)
    return np.ascontiguousarray(t.transpose(3, 0, 2, 1, 4).reshape(4, nk // 4, 128, 4 * 512))


NFMB = 16
NFMF = 28
FM_COLS = ([0 + 128 * i for i in range(4)] + [512 + 128 * i for i in range(4)] +
           [1536 + 128 * i for i in range(4)] + [2048 + 128 * i for i in range(4)] +
           [3072 + 128 * i for i in range(4)] + [3584 + 128 * i for i in range(4)] +
           [4608 + 128 * i for i in range(4)] + [5120 + 128 * i for i in range(12)] +
           [6656 + 128 * i for i in range(4)])
TM_COLS = [1024, 2560, 4096]
AB_COL = 7168
QSCALE = HD ** -0.5


def lay_win(w_in):
    cols = [w_in[:, c:c + 128].reshape(KC, 128, 128) for c in FM_COLS]
    fm = np.stack(cols, axis=0).reshape(22, 2, KC, 128, 128)
    fm = np.ascontiguousarray(fm.transpose(0, 3, 2, 1, 4).reshape(22, 128, KC * 256))
    tm = []
    for c in TM_COLS:
        w = w_in[:, c:c + 512].reshape(4, 4, 128, 512)
        tm.append(w.transpose(0, 2, 1, 3).reshape(4, 128, 4 * 512))
    tm = np.ascontiguousarray(np.stack(tm, axis=0))
    ab = np.zeros((128, KC, 128), np.float32)
    ab[:, :, 0:8] = w_in[:, AB_COL:AB_COL + 8].reshape(KC, 128, 8).transpose(1, 0, 2)
    return fm, tm, np.ascontiguousarray(ab.reshape(128, KC * 128))


def setup_proj(S, es):
    k = S.k
    S.stb = Ring([(k.sb(f"stb_{i}", [128, TT], BF16), Buf(f"stb_{i}"), k.slot(f"stb_{i}")) for i in range(3)])
    S.stf = Ring([(k.sb(f"stf_{i}", [128, TT], F32), Buf(f"stf_{i}"), k.slot(f"stf_{i}")) for i in range(3)])
    S.vt = Ring([(k.sb(f"vt_{i}", [128, 3 * 512], BF16), Buf(f"vt_{i}"), k.slot(f"vt_{i}")) for i in range(4)])
    S.wab = k.sb("wab_sb", [128, KC * 128], BF16)
    S.b_wab = Buf("wab")
    S.s_wab = k.slot("wab")
    S.abst = k.sb("abst", [8, TT], F32)
    S.b_abst = Buf("abst")
    S.s_abst = k.slot("abst")


def proj_tile(S, t0, wfm_tile, b_wfm, wtm_tile, b_wtm, FMb, FMf, TMv, AB, b_pr):
    import os
    PSTOP = int(os.environ.get('PSTOP', '9'))
    k = S.k
    if PSTOP <= 0:
        return
    for pr in range(22):
        wt, b_wt, s_wt = S.w13.next()
        k.dma(k.sp, s_wt, wt[:], wfm_tile(pr), reads=[b_wfm], writes=[b_wt])
        for w in range(2):
            c = pr * 2 + w
            pb, b_pb = bank(S)
            fns = [(lambda e, kc=kc: e.matmul(pb[:, :], lhsT=wt[:, kc * 256 + w * 128: kc * 256 + (w + 1) * 128],
                                              rhs=S.hT[:, kc * TT:(kc + 1) * TT], start=(kc == 0), stop=(kc == KC - 1)))
                   for kc in range(KC)]
            k.group(k.pe, fns, reads=[b_wt, S.b_hT], writes=[b_pb])
            if c < NFMB:
                st, b_st, s_st = S.stb.next()
                sc = QSCALE if (c < 4 or 8 <= c < 12) else 1.0
                dst = FMb[c, :, t0:t0 + TT]
            else:
                st, b_st, s_st = S.stf.next()
                sc = 1.0
                dst = FMf[c - NFMB, :, t0:t0 + TT]
            if c % 2 == 0:
                k.op(k.act, lambda e: e.activation(out=st[:], in_=pb[:, :], func=AF.Copy, scale=float(sc)),
                     reads=[b_pb], writes=[b_st])
            else:
                k.op(k.dve, lambda e: e.tensor_scalar(out=st[:], in0=pb[:, :], scalar1=float(sc), scalar2=None, op0=ALU.mult),
                     reads=[b_pb], writes=[b_st])
            k.dma(k.pool, s_st, dst, st[:], reads=[b_st], writes=[b_pr])
    if PSTOP <= 1:
        return
    for cb in range(3):
        banks = [bank(S) for _ in range(4)]
        for jg in range(4):
            wt, b_wt, s_wt = S.w2.next()
            k.dma(k.sp, s_wt, wt[:], wtm_tile(cb, jg), reads=[b_wtm], writes=[b_wt])
            fns = []
            for jj in range(4):
                j = jg * 4 + jj
                for tb in range(4):
                    fns.append(lambda e, j=j, jj=jj, tb=tb: e.matmul(
                        banks[tb][0][:, :], lhsT=S.hT[:, j * TT + tb * 128: j * TT + (tb + 1) * 128],
                        rhs=wt[:, jj * 512:(jj + 1) * 512], start=(j == 0), stop=(j == KC - 1)))
            k.group(k.pe, fns, reads=[b_wt, S.b_hT], writes=[b for _, b in banks])
        for tb in range(4):
            vt, b_vt, s_vt = S.vt.items[tb]
            if tb % 2 == 0:
                k.op(k.act, lambda e: e.activation(out=vt[:, cb * 512:(cb + 1) * 512], in_=banks[tb][0][:, :], func=AF.Copy),
                     reads=[banks[tb][1]], writes=[b_vt])
            else:
                k.op(k.dve, lambda e: e.tensor_copy(out=vt[:, cb * 512:(cb + 1) * 512], in_=banks[tb][0][:, :]),
                     reads=[banks[tb][1]], writes=[b_vt])
    for tb in range(4):
        vt, b_vt, s_vt = S.vt.items[tb]
        k.dma(k.pool, s_vt, TMv[:, t0 + tb * 128: t0 + (tb + 1) * 128, :].rearrange("c p f -> p c f"),
              vt[:].rearrange("p (c f) -> p c f", c=3), reads=[b_vt], writes=[b_pr])
    if PSTOP <= 2:
        return
    pb, b_pb = bank(S)
    fns = [(lambda e, kc=kc: e.matmul(pb[:, :], lhsT=S.wab[:, kc * 128:(kc + 1) * 128],
                                      rhs=S.hT[:, kc * TT:(kc + 1) * TT], start=(kc == 0), stop=(kc == KC - 1)))
           for kc in range(KC)]
    k.group(k.pe, fns, reads=[S.b_wab, S.b_hT], writes=[b_pb])
    k.op(k.dve, lambda e: e.tensor_copy(out=S.abst[:], in_=pb[0:8, :]), reads=[b_pb], writes=[S.b_abst])
    k.dma(k.pool, S.s_abst, AB[:, t0:t0 + TT], S.abst[:], reads=[S.b_abst], writes=[b_pr])


def mixer_consts():
    c = {}
    p = np.arange(128)[:, None]
    f = np.arange(512)[None, :]
    am = np.zeros((20, 128, 512), np.float32)
    for m in range(20):
        d = f - p - 128 * (m - 16)
        cnt = ((d >= 0) & (d <= 128)).astype(np.float32)
        cnt += ((d >= 0) & (d <= 512) & (d % 4 == 0))
        cnt += ((d >= 0) & (d <= 2048) & (d % 16 == 0))
        am[m] = cnt
    c["amask"] = am.astype(ml_dtypes.bfloat16)
    bm = np.zeros((4, 128, 512), np.float32)
    for i in range(4):
        bm[i] = (p + 128 * i < f)
    c["bmask"] = bm.astype(ml_dtypes.bfloat16)
    j = np.arange(128)[:, None]
    s = np.arange(128)[None, :]
    c["negtri"] = (-(j >= s).astype(np.float32)).astype(ml_dtypes.bfloat16)
    c["ones"] = np.ones((128, 128), ml_dtypes.bfloat16)
    return c


class Pool2:
    def __init__(self, S, idxs):
        self.S, self.idxs, self.i = S, idxs, 0

    def next(self):
        i = self.idxs[self.i % len(self.idxs)]
        self.i += 1
        return self.S.ps[i], self.S.psb[i]


def setup_mix_common(S, cst, og_ap, S_len):
    k = S.k
    S.L = S_len
    S.ones_bf = k.sb("ones_bf", [128, 128], BF16)
    S.negtri = k.sb("negtri", [128, 128], BF16)
    S.one_col = k.sb("one_col", [128, 1], F32)
    S.ogP = k.sb("ogP", [128, 16], F32)
    S.b_mc = Buf("mixconst")
    sl = k.slot("mixconst")
    k.dma(k.sp, sl, S.ones_bf[:], cst["ones"], writes=[S.b_mc])
    k.dma(k.sp, sl, S.negtri[:], cst["negtri"], writes=[S.b_mc])
    k.dma(k.sp, sl, S.ogP[:].rearrange("p (g h) -> p g h", h=4), og_ap.rearrange("g (h p) -> p g h", p=128),
          writes=[S.b_mc], allow_slow_non_contiguous=True)
    k.op(k.dve, lambda e: e.memset(S.one_col[:], 1.0), writes=[S.b_mc])
    S.hp_sq = Ring([(k.sb(f"hp_sq{i}", [128, TT], BF16), Buf(f"hp_sq{i}")) for i in range(2)])
    S.hp_sd = Ring([(k.sb(f"hp_sd{i}", [128, TT], F32), Buf(f"hp_sd{i}")) for i in range(2)])
    S.hp_g = Ring([(k.sb(f"hp_g{i}", [128, TT], F32), Buf(f"hp_g{i}"), k.slot(f"hp_g{i}")) for i in range(2)])
    S.hp_o = Ring([(k.sb(f"hp_o{i}", [128, TT], BF16), Buf(f"hp_o{i}"), k.slot(f"hp_o{i}")) for i in range(2)])
    S.oT = Ring([(k.sb(f"oT{i}", [128, TT], F32), Buf(f"oT{i}")) for i in range(2)])


def head_post(S, oT, b_oT, hg, t0, n, mixedT, b_mixed, gate_ap=None, b_gate=None):
    k = S.k
    sq, b_sq = S.hp_sq.next()
    k.op(k.act, lambda e: e.activation(out=sq[:, 0:n], in_=oT, func=AF.Square), reads=[b_oT], writes=[b_sq])
    pb, b_pb = S.rot.next()
    k.op(k.pe, lambda e: e.matmul(pb[:, 0:n], lhsT=S.ones_bf[:], rhs=sq[:, 0:n], start=True, stop=True),
         reads=[b_sq, S.b_mc], writes=[b_pb])
    sd, b_sd = S.hp_sd.next()
    k.op(k.act, lambda e: e.activation(out=sd[:, 0:n], in_=pb[:, 0:n], func=AF.Sqrt, scale=1.0 / HD, bias=S.eps_col[:, 0:1]),
         reads=[b_pb, S.b_const], writes=[b_sd])
    k.op(k.dve, lambda e: e.reciprocal(out=sd[:, 0:n], in_=sd[:, 0:n]), reads=[b_sd], writes=[b_sd])
    ho, b_ho, s_ho = S.hp_o.next()
    if gate_ap is None:
        k.op(k.dve, lambda e: e.scalar_tensor_tensor(out=ho[:, 0:n], in0=oT, scalar=S.ogP[:, hg:hg + 1], in1=sd[:, 0:n],
                                                      op0=ALU.mult, op1=ALU.mult),
             reads=[b_oT, b_sd, S.b_mc], writes=[b_ho])
    else:
        gt, b_gt, s_gt = S.hp_g.next()
        k.dma(k.sp, s_gt, gt[:, 0:n], gate_ap, reads=[b_gate], writes=[b_gt])
        k.op(k.act, lambda e: e.activation(out=gt[:, 0:n], in_=gt[:, 0:n], func=AF.Silu), reads=[b_gt], writes=[b_gt])
        k.op(k.dve, lambda e: e.scalar_tensor_tensor(out=sd[:, 0:n], in0=oT, scalar=S.ogP[:, hg:hg + 1], in1=sd[:, 0:n],
                                                      op0=ALU.mult, op1=ALU.mult),
             reads=[b_oT, b_sd, S.b_mc], writes=[b_sd])
        k.op(k.pool, lambda e: e.tensor_tensor(out=ho[:, 0:n], in0=sd[:, 0:n], in1=gt[:, 0:n], op=ALU.mult),
             reads=[b_sd, b_gt], writes=[b_ho])
    return k.dma(k.pool, s_ho, mixedT[hg * 128:(hg + 1) * 128, t0:t0 + n], ho[:, 0:n], reads=[b_ho], writes=[b_mixed])


def setup_attn(S, cst):
    k = S.k
    L = S.L
    S.QT = k.sb("QT", [128, L], BF16)
    S.KT = k.sb("KT", [128, L], BF16)
    S.V = k.sb("Vtm", [128, L], BF16)
    S.b_qkv = Buf("qkv")
    S.s_qkv = k.slot("qkv")
    S.amask = k.sb("amask", [128, 20 * 512], BF16)
    S.bmask = k.sb("bmask", [128, 4 * 512], BF16)
    sl = k.slot("masks")
    k.dma(k.sp, sl, S.amask[:].rearrange("p (m f) -> p m f", f=512), cst["amask"].rearrange("m p f -> p m f"), writes=[S.b_mc])
    k.dma(k.sp, sl, S.bmask[:].rearrange("p (m f) -> p m f", f=512), cst["bmask"].rearrange("m p f -> p m f"), writes=[S.b_mc])
    S.E = Ring([(k.sb(f"E{i}", [128, TT], F32), Buf(f"E{i}")) for i in range(2)])
    S.Lb = Ring([(k.sb(f"Lb{i}", [128, TT], BF16), Buf(f"Lb{i}")) for i in range(2)])
    S.W = Ring([(k.sb(f"W{i}", [128, TT], F32), Buf(f"W{i}")) for i in range(2)])
    S.P = Ring([(k.sb(f"P{i}", [128, TT], BF16), Buf(f"P{i}")) for i in range(3)])
    S.carry = k.sb("carry", [128, TT], F32)
    S.b_carry = Buf("carry")


def load_head_qkv(S, FMb, iq, ik, TMv, iv, h, b_pr):
    k = S.k
    L = S.L
    k.dma(k.sp, S.s_qkv, S.QT[:], FMb[iq], reads=[b_pr], writes=[S.b_qkv])
    k.dma(k.sp, S.s_qkv, S.KT[:], FMb[ik], reads=[b_pr], writes=[S.b_qkv])
    for c0 in range(0, L, 1024):
        c1 = min(L, c0 + 1024)
        k.dma(k.sp, S.s_qkv, S.V[:, c0:c1].rearrange("p (n d) -> p n d", d=128),
              TMv[iv, c0:c1, h * 128:(h + 1) * 128].rearrange("(n p) d -> p n d", p=128), reads=[b_pr], writes=[S.b_qkv])


def mixer_A_head(S, hg, mixedT, b_mixed):
    k = S.k
    evs = []
    for qg in range(S.L // TT):
        t0 = qg * TT
        O, b_O = S.acc.next()
        Dn, b_Dn = S.acc.next()
        blocks = list(range(max(0, t0 - 2048), t0 + TT, 128))
        for idx, s0 in enumerate(blocks):
            m = (s0 - t0) // 128 + 16
            Z, b_Z = S.rot.next()
            k.op(k.pe, lambda e: e.matmul(Z[:, :], lhsT=S.KT[:, s0:s0 + 128], rhs=S.QT[:, t0:t0 + TT], start=True, stop=True),
                 reads=[S.b_qkv], writes=[b_Z])
            P, b_P = S.P.next()
            k.op(k.act, lambda e: e.activation(out=P[:], in_=Z[:, :], func=AF.Exp), reads=[b_Z], writes=[b_P])
            k.op(k.pool, lambda e: e.tensor_tensor(out=P[:], in0=P[:], in1=S.amask[:, m * 512:(m + 1) * 512], op=ALU.mult),
                 reads=[S.b_mc], writes=[b_P])
            k.group(k.pe, [
                lambda e: e.matmul(O[:, :], lhsT=S.V[:, s0:s0 + 128], rhs=P[:], start=(idx == 0), stop=(idx == len(blocks) - 1)),
                lambda e: e.matmul(Dn[:, :], lhsT=S.ones_bf[:], rhs=P[:], start=(idx == 0), stop=(idx == len(blocks) - 1)),
            ], reads=[b_P, S.b_qkv, S.b_mc], writes=[b_O, b_Dn])
        W, b_W = S.W.next()
        k.op(k.dve, lambda e: e.reciprocal(out=W[:], in_=Dn[:, :]), reads=[b_Dn], writes=[b_W])
        oT, b_oT = S.oT.next()
        k.op(k.dve, lambda e: e.tensor_tensor(out=oT[:], in0=O[:, :], in1=W[:], op=ALU.mult), reads=[b_O, b_W], writes=[b_oT])
        evs.append(head_post(S, oT[:], b_oT, hg, t0, TT, mixedT, b_mixed))
    return evs


def mixer_B_head(S, hg, mixedT, b_mixed):
    k = S.k
    evs = []
    for qg in range(S.L // TT):
        t0 = qg * TT
        O, b_O = S.acc.next()
        k.op(k.pool, lambda e: e.memset(S.carry[:], 0.0), writes=[S.b_carry])
        blocks = list(range(t0 + TT - 128, -1, -128))
        for idx, s0 in enumerate(blocks):
            diag = s0 >= t0
            mi = (s0 - t0) // 128
            Z, b_Z = S.rot.next()
            k.op(k.pe, lambda e: e.matmul(Z[:, :], lhsT=S.KT[:, s0:s0 + 128], rhs=S.QT[:, t0:t0 + TT], start=True, stop=True),
                 reads=[S.b_qkv], writes=[b_Z])
            E, b_E = S.E.next()
            k.op(k.act, lambda e: e.activation(out=E[:], in_=Z[:, :], func=AF.Exp), reads=[b_Z], writes=[b_E])
            Lb, b_Lb = S.Lb.next()
            k.op(k.act, lambda e: e.activation(out=Lb[:], in_=E[:], func=AF.Ln, bias=S.one_col[:, 0:1]),
                 reads=[b_E, S.b_mc], writes=[b_Lb])
            if diag:
                k.op(k.pool, lambda e: e.tensor_tensor(out=Lb[:], in0=Lb[:], in1=S.bmask[:, mi * 512:(mi + 1) * 512], op=ALU.mult),
                     reads=[S.b_mc], writes=[b_Lb])
            T, b_T = S.rot.next()
            C, b_C = S.rot.next()
            k.group(k.pe, [
                lambda e: e.matmul(T[:, :], lhsT=S.KT[:, s0:s0 + 128], rhs=S.QT[:, t0:t0 + TT], start=True, stop=False),
                lambda e: e.matmul(T[:, :], lhsT=S.negtri[:], rhs=Lb[:], start=False, stop=True),
                lambda e: e.matmul(C[:, :], lhsT=S.ones_bf[:], rhs=Lb[:], start=True, stop=True),
            ], reads=[b_Lb, S.b_qkv, S.b_mc], writes=[b_T, b_C])
            W, b_W = S.W.next()
            k.op(k.dve, lambda e: e.tensor_tensor(out=W[:], in0=T[:, :], in1=S.carry[:], op=ALU.subtract),
                 reads=[b_T, S.b_carry], writes=[b_W])
            k.op(k.dve, lambda e: e.tensor_tensor(out=S.carry[:], in0=C[:, :], in1=S.carry[:], op=ALU.add),
                 reads=[b_C], writes=[S.b_carry])
            P, b_P = S.P.next()
            k.op(k.act, lambda e: e.activation(out=P[:], in_=W[:], func=AF.Exp), reads=[b_W], writes=[b_P])
            if diag:
                k.op(k.pool, lambda e: e.tensor_tensor(out=P[:], in0=P[:], in1=S.bmask[:, mi * 512:(mi + 1) * 512], op=ALU.mult),
                     reads=[S.b_mc], writes=[b_P])
            k.op(k.pe, lambda e: e.matmul(O[:, :], lhsT=S.V[:, s0:s0 + 128], rhs=P[:], start=(idx == 0), stop=(idx == len(blocks) - 1)),
                 reads=[b_P, S.b_qkv], writes=[b_O])
        oT, b_oT = S.oT.next()
        k.op(k.act, lambda e: e.activation(out=oT[:], in_=O[:, :], func=AF.Copy), reads=[b_O], writes=[b_oT])
        evs.append(head_post(S, oT[:], b_oT, hg, t0, TT, mixedT, b_mixed))
    return evs


CC = 32
CD = 64


def mixer_cd_host(lb_logits, conv_w, a_log, dt_bias):
    d = {}
    d["lgP"] = np.ascontiguousarray(lb_logits.reshape(DEPTH, GH, 128).transpose(2, 0, 1).reshape(128, DEPTH * GH))
    d["cwP"] = np.ascontiguousarray(conv_w.reshape(4, 3, GH, 128).transpose(3, 1, 2, 0).reshape(128, 3 * GH * 4))
    d["alog"] = np.ascontiguousarray(a_log.reshape(1, GH))
    d["dtb"] = np.ascontiguousarray(dt_bias.reshape(1, GH))
    t = np.arange(TT)
    d["scan32"] = np.broadcast_to((t % CC != 0).astype(np.float32)[None, :], (128, TT)).copy()
    d["scan64"] = (t % CD != 0).astype(np.float32)[None, :].copy()
    p = np.arange(128)[:, None]
    d["mask32"] = ((p % CC) <= np.arange(CC)[None, :]).astype(np.float32).astype(ml_dtypes.bfloat16)
    s = np.arange(CD)[:, None]
    u = np.arange(CD)[None, :]
    d["maskSL"] = (u < s).astype(np.float32)
    d["maskUI"] = (s <= u).astype(np.float32)
    d["ident64"] = np.eye(CD, dtype=np.float32)
    return d


CD_SHAPES = {"lgP": ([128, DEPTH * GH], F32), "cwP": ([128, 3 * GH * 4], F32), "alog": ([1, GH], F32), "dtb": ([1, GH], F32),
             "scan32": ([128, TT], F32), "scan64": ([1, TT], F32), "mask32": ([128, CC], BF16),
             "maskSL": ([CD, CD], F32), "maskUI": ([CD, CD], F32), "ident64": ([CD, CD], F32)}


def mixer_cd_inputs(nc):
    return {n: nc.dram_tensor(n, sh, dt, kind="ExternalInput").ap() for n, (sh, dt) in CD_SHAPES.items()}


def mixer_C(S, FMf, TMv, ex, mixedT, b_pr, b_mixed, layer):
    k = S.k
    L = S.L
    S.acc = Pool2(S, [0, 1])
    S.rot = Pool2(S, [2, 3, 4, 5, 6, 7])
    sl = k.slot("c_const")
    b_cc = Buf("c_const")
    lgP = k.sb("lgP", [128, DEPTH * GH], F32)
    scan32 = k.sb("scan32", [128, TT], F32)
    mask32 = k.sb("mask32", [128, CC], BF16)
    k.dma(k.sp, sl, lgP[:], ex["lgP"], writes=[b_cc])
    k.dma(k.sp, sl, scan32[:], ex["scan32"], writes=[b_cc])
    k.dma(k.sp, sl, mask32[:], ex["mask32"], writes=[b_cc])
    lbt = k.sb("lbt", [128, GH], F32)
    oml = k.sb("oml", [128, GH], F32)
    ssum = k.sb("ssum", [128, GH], F32)
    k.op(k.act, lambda e: e.activation(out=lgP[:], in_=lgP[:], func=AF.Exp), reads=[b_cc], writes=[b_cc])
    k.op(k.dve, lambda e: e.tensor_tensor(out=ssum[:], in0=lgP[:, 0:GH], in1=lgP[:, GH:2 * GH], op=ALU.add), reads=[b_cc], writes=[b_cc])
    for l2 in range(2, DEPTH):
        k.op(k.dve, lambda e: e.tensor_tensor(out=ssum[:], in0=ssum[:], in1=lgP[:, l2 * GH:(l2 + 1) * GH], op=ALU.add), reads=[b_cc], writes=[b_cc])
    k.op(k.dve, lambda e: e.reciprocal(out=ssum[:], in_=ssum[:]), reads=[b_cc], writes=[b_cc])
    k.op(k.dve, lambda e: e.memset(lbt[:], 0.0), writes=[b_cc])
    for l2 in range(1, layer + 1):
        k.op(k.dve, lambda e: e.tensor_tensor(out=lbt[:], in0=lbt[:], in1=lgP[:, l2 * GH:(l2 + 1) * GH], op=ALU.add), reads=[b_cc], writes=[b_cc])
    k.op(k.dve, lambda e: e.tensor_tensor(out=lbt[:], in0=lbt[:], in1=ssum[:], op=ALU.mult), reads=[b_cc], writes=[b_cc])
    k.op(k.dve, lambda e: e.tensor_scalar(out=oml[:], in0=lbt[:], scalar1=-1.0, scalar2=1.0, op0=ALU.mult, op1=ALU.add),
         reads=[b_cc], writes=[b_cc])

    fr = Ring([(k.sb(f"c_fr{i}", [128, TT], F32), Buf(f"c_fr{i}"), k.slot(f"c_fr{i}")) for i in range(2)])
    qr = Ring([(k.sb(f"c_qr{i}", [128, TT], F32), Buf(f"c_qr{i}"), k.slot(f"c_qr{i}")) for i in range(2)])
    vr = Ring([(k.sb(f"c_v{i}", [64, 8 * 128], BF16), Buf(f"c_v{i}"), k.slot(f"c_v{i}")) for i in range(2)])
    bt = Ring([(k.sb(f"c_b{i}", [128, TT], F32), Buf(f"c_b{i}")) for i in range(2)])
    ebr = Ring([(k.sb(f"c_eb{i}", [128, TT], F32), Buf(f"c_eb{i}")) for i in range(2)])
    enb = Ring([(k.sb(f"c_enb{i}", [128, TT], F32), Buf(f"c_enb{i}")) for i in range(2)])
    kpT = Ring([(k.sb(f"c_kp{i}", [128, TT], BF16), Buf(f"c_kp{i}")) for i in range(2)])
    kppT = Ring([(k.sb(f"c_kpp{i}", [128, TT], BF16), Buf(f"c_kpp{i}")) for i in range(2)])
    qpT = Ring([(k.sb(f"c_qp{i}", [128, TT], BF16), Buf(f"c_qp{i}")) for i in range(2)])
    ktm = Ring([(k.sb(f"c_ktm{i}", [64, 8 * 128], BF16), Buf(f"c_ktm{i}")) for i in range(2)])
    stm = Ring([(k.sb(f"c_stm{i}", [128, CC], BF16), Buf(f"c_stm{i}")) for i in range(4)])
    Sf = k.sb("c_Sf", [128, 128], F32)
    b_Sf = Buf("c_Sf")
    Sb = Ring([(k.sb(f"c_Sb{i}", [128, 128], BF16), Buf(f"c_Sb{i}")) for i in range(2)])
    evs = []
    for h in range(GH):
        k.op(k.dve, lambda e: e.memset(Sf[:], 0.0), writes=[b_Sf])
        sb_cur, b_sb_cur = Sb.next()
        k.op(k.pool, lambda e: e.memset(sb_cur[:], 0.0), writes=[b_sb_cur])
        for g in range(L // TT):
            t0 = g * TT
            f_, b_f, s_f = fr.next()
            q_, b_q, s_q = qr.next()
            v_, b_v, s_v = vr.next()
            k.dma(k.sp, s_f, f_[:], FMf[4 + h, :, t0:t0 + TT], reads=[b_pr], writes=[b_f])
            k.dma(k.sp, s_q, q_[:], FMf[0 + h, :, t0:t0 + TT], reads=[b_pr], writes=[b_q])
            k.dma(k.sp, s_v, v_[:].rearrange("p (n d) -> p n d", d=128),
                  TMv[2, t0:t0 + TT, h * 128:(h + 1) * 128].rearrange("(n p) d -> p n d", p=64), reads=[b_pr], writes=[b_v])
            k.op(k.act, lambda e: e.activation(out=f_[:], in_=f_[:], func=AF.Sigmoid), reads=[b_f], writes=[b_f])
            k.op(k.dve, lambda e: e.tensor_scalar(out=f_[:], in0=f_[:], scalar1=oml[:, h:h + 1], scalar2=lbt[:, h:h + 1],
                                                  op0=ALU.mult, op1=ALU.add), reads=[b_f, b_cc], writes=[b_f])
            b_, b_b = bt.next()
            k.op(k.act, lambda e: e.activation(out=b_[:], in_=f_[:], func=AF.Ln), reads=[b_f], writes=[b_b])
            k.op(k.dve, lambda e: e.tensor_tensor_scan(out=b_[:], data0=scan32[:], data1=b_[:], initial=0.0,
                                                       op0=ALU.mult, op1=ALU.add), reads=[b_cc], writes=[b_b])
            eb, b_eb = ebr.next()
            en, b_en = enb.next()
            k.op(k.act, lambda e: e.activation(out=eb[:], in_=b_[:], func=AF.Exp), reads=[b_b], writes=[b_eb])
            k.op(k.act, lambda e: e.activation(out=en[:], in_=b_[:], func=AF.Exp, scale=-1.0), reads=[b_b], writes=[b_en])
            kp, b_kp = kpT.next()
            k.op(k.pool, lambda e: e.tensor_scalar(out=f_[:], in0=f_[:], scalar1=-1.0, scalar2=1.0, op0=ALU.mult, op1=ALU.add),
                 reads=[b_b], writes=[b_f])
            k.op(k.dve, lambda e: e.tensor_tensor(out=kp[:], in0=f_[:], in1=en[:], op=ALU.mult), reads=[b_f, b_en], writes=[b_kp])
            kpp, b_kpp = kppT.next()
            ebl = eb[:, CC - 1:TT:CC].unsqueeze(2).broadcast_to([128, TT // CC, CC])
            k.op(k.dve, lambda e: e.tensor_tensor(out=kpp[:].rearrange("p (c t) -> p c t", t=CC),
                                                  in0=kp[:].rearrange("p (c t) -> p c t", t=CC), in1=ebl, op=ALU.mult),
                 reads=[b_kp, b_eb], writes=[b_kpp])
            k.op(k.act, lambda e: e.activation(out=q_[:], in_=q_[:], func=AF.Silu), reads=[b_q], writes=[b_q])
            qp, b_qp = qpT.next()
            k.op(k.pool, lambda e: e.tensor_tensor(out=qp[:], in0=q_[:], in1=eb[:], op=ALU.mult), reads=[b_q, b_eb], writes=[b_qp])
            pt, b_pt = S.rot.next()
            ptb = pt[:].bitcast(BF16)
            k.group(k.pe, [(lambda e, n=n: e.transpose(out=ptb[0:64, n * 128:(n + 1) * 128], in_=kpp[:, n * 64:(n + 1) * 64],
                                                        identity=S.ident_b[:])) for n in range(8)],
                    reads=[b_kpp, S.b_const], writes=[b_pt])
            kt, b_kt = ktm.next()
            k.op(k.act, lambda e: e.activation(out=kt[:], in_=ptb[0:64, :], func=AF.Copy), reads=[b_pt], writes=[b_kt])
            O, b_O = S.acc.next()
            for c in range(TT // CC):
                n, r0 = c // 2, (c % 2) * CC
                cs = slice(c * CC, (c + 1) * CC)
                Z, b_Z = S.rot.next()
                k.op(k.pe, lambda e: e.matmul(Z[r0:r0 + CC, 0:CC], lhsT=kp[:, cs], rhs=qp[:, cs], start=True, stop=True),
                     reads=[b_kp, b_qp], writes=[b_Z])
                st, b_st = stm.next()
                k.op(k.dve, lambda e: e.tensor_tensor(out=st[r0:r0 + CC, :], in0=Z[r0:r0 + CC, 0:CC], in1=mask32[r0:r0 + CC, :],
                                                      op=ALU.mult), reads=[b_Z, b_cc], writes=[b_st])
                sb_prev, b_sb_prev = sb_cur, b_sb_cur
                k.group(k.pe, [
                    lambda e: e.matmul(O[:, cs], lhsT=sb_prev[:], rhs=qp[:, cs], start=True, stop=False),
                    lambda e: e.matmul(O[:, cs], lhsT=v_[r0:r0 + CC, n * 128:(n + 1) * 128], rhs=st[r0:r0 + CC, :], start=False, stop=True),
                ], reads=[b_sb_prev, b_qp, b_v, b_st], writes=[b_O])
                SU, b_SU = S.rot.next()
                k.op(k.pe, lambda e: e.matmul(SU[:, 0:128], lhsT=kt[r0:r0 + CC, n * 128:(n + 1) * 128],
                                              rhs=v_[r0:r0 + CC, n * 128:(n + 1) * 128], start=True, stop=True),
                     reads=[b_kt, b_v], writes=[b_SU])
                k.op(k.dve, lambda e: e.scalar_tensor_tensor(out=Sf[:], in0=Sf[:], scalar=eb[:, c * CC + CC - 1: c * CC + CC],
                                                              in1=SU[:, 0:128], op0=ALU.mult, op1=ALU.add),
                     reads=[b_SU, b_eb], writes=[b_Sf])
                sb_cur, b_sb_cur = Sb.next()
                k.op(k.act, lambda e: e.activation(out=sb_cur[:], in_=Sf[:], func=AF.Copy), reads=[b_Sf], writes=[b_sb_cur])
            oT, b_oT = S.oT.next()
            k.op(k.act, lambda e: e.activation(out=oT[:], in_=O[:, :], func=AF.Copy), reads=[b_O], writes=[b_oT])
            evs.append(head_post(S, oT[:], b_oT, 8 + h, t0, TT, mixedT, b_mixed, gate_ap=FMf[8 + h, :, t0:t0 + TT], b_gate=b_pr))
    return evs


def mixer_D(S, FMf, AB, ex, mixedT, b_pr, b_mixed):
    import os
    DSTOP = int(os.environ.get('DSTOP', '9'))
    DROWS = int(os.environ.get('DROWS', '99'))
    k = S.k
    L = S.L
    S.acc = Pool2(S, [0, 1])
    S.rot = Pool2(S, [2, 3, 4, 5, 6, 7])
    sl = k.slot("d_const")
    b_dc = Buf("d_const")
    cwP = k.sb("cwP", [128, 3 * GH * 4], F32)
    alog = k.sb("alog", [1, GH], F32)
    dtb = k.sb("dtb", [1, GH], F32)
    scan64 = k.sb("scan64", [1, TT], F32)
    maskSL = k.sb("maskSL", [CD, CD], F32)
    maskUI = k.sb("maskUI", [CD, CD], F32)
    id64 = k.sb("id64", [CD, CD], F32)
    ones_row = k.sb("ones_row", [1, 128], F32)
    for t_, n_ in [(cwP, "cwP"), (alog, "alog"), (dtb, "dtb"), (scan64, "scan64"), (maskSL, "maskSL"), (maskUI, "maskUI"), (id64, "ident64")]:
        k.dma(k.sp, sl, t_[:], ex[n_], writes=[b_dc])
    k.op(k.dve, lambda e: e.memset(ones_row[:], 1.0), writes=[b_dc])
    k.op(k.act, lambda e: e.activation(out=alog[:], in_=alog[:], func=AF.Exp), reads=[b_dc], writes=[b_dc])
    k.op(k.dve, lambda e: e.tensor_scalar(out=alog[:], in0=alog[:], scalar1=-1.0, scalar2=None, op0=ALU.mult), reads=[b_dc], writes=[b_dc])

    def ring(name, shape, dt, n=2, slot=False):
        if slot:
            return Ring([(k.sb(f"{name}{i}", shape, dt), Buf(f"{name}{i}"), k.slot(f"{name}{i}")) for i in range(n)])
        return Ring([(k.sb(f"{name}{i}", shape, dt), Buf(f"{name}{i}")) for i in range(n)])

    xin = [ring(f"d_x{a}", [128, TT + 3], F32, 2, True) for a in range(3)]
    cv = [ring(f"d_cv{a}", [128, TT], F32, 2) for a in range(3)]
    sqb = ring("d_sq", [128, TT], BF16, 2)
    rsd = ring("d_rs", [128, TT], F32, 2)
    qnb = ring("d_qnb", [128, TT], BF16, 2)
    knb = ring("d_knb", [128, TT], BF16, 2)
    qgb = ring("d_qgb", [128, TT], BF16, 2)
    Gb = ring("d_Gb", [128, TT], F32, 2)
    Eb = ring("d_Eb", [128, TT], F32, 2)
    ra = ring("d_ra", [1, TT], F32, 2, True)
    rb = ring("d_rb", [1, TT], F32, 2, True)
    rg = ring("d_rg", [1, TT], F32, 2)
    rbeg = ring("d_rbeg", [1, TT], F32, 2)
    red = ring("d_red", [1, TT], F32, 2)
    colS = ring("d_col", [CD, 4], F32, 3)
    rel = ring("d_rel", [CD, CD], F32, 3)
    relT = ring("d_relT", [CD, CD], F32, 3)
    Nm = ring("d_N", [CD, CD], F32, 8)
    TTr = ring("d_TT", [CD, CD], F32, 4)
    QKm = ring("d_QKm", [CD, CD], BF16, 3)
    ktok = ring("d_ktok", [CD, 128], F32, 3)
    kdb = ring("d_kd", [CD, 128], BF16, 3)
    bv = ring("d_bv", [CD, 128], F32, 3)
    ut = ring("d_u", [CD, 128], F32, 3)
    wTb = ring("d_wT", [128, CD], BF16, 3)
    vnew = ring("d_vnew", [CD, 128], BF16, 3)
    Sf = k.sb("d_Sf", [128, 128], F32)
    b_Sf = Buf("d_Sf")
    Sb = ring("d_Sb", [128, 128], BF16, 2)
    evs = []
    for h in range(GH):
        k.op(k.dve, lambda e: e.memset(Sf[:], 0.0), writes=[b_Sf])
        sb_cur, b_sb_cur = Sb.next()
        k.op(k.pool, lambda e: e.memset(sb_cur[:], 0.0), writes=[b_sb_cur])
        for g in range(L // TT):
            t0 = g * TT
            if DSTOP <= -1:
                continue
            cvs = []
            for a in range(3):
                x_, b_x, s_x = xin[a].next()
                if t0 == 0:
                    k.op(k.pool, lambda e: e.memset(x_[:, 0:3], 0.0), writes=[b_x])
                    k.dma(k.sp, s_x, x_[:, 3:TT + 3], FMf[12 + 4 * a + h, :, 0:TT], reads=[b_pr], writes=[b_x])
                else:
                    k.dma(k.sp, s_x, x_[:, :], FMf[12 + 4 * a + h, :, t0 - 3:t0 + TT], reads=[b_pr], writes=[b_x])
                c_, b_c = cv[a].next()
                wbase = (a * GH + h) * 4
                k.op(k.dve, lambda e: e.tensor_scalar(out=c_[:], in0=x_[:, 0:TT], scalar1=cwP[:, wbase:wbase + 1], scalar2=None, op0=ALU.mult),
                     reads=[b_x, b_dc], writes=[b_c])
                for tap in range(1, 4):
                    k.op(k.dve, lambda e, tap=tap: e.scalar_tensor_tensor(out=c_[:], in0=x_[:, tap:TT + tap], scalar=cwP[:, wbase + tap:wbase + tap + 1],
                                                                          in1=c_[:], op0=ALU.mult, op1=ALU.add), reads=[b_x, b_dc], writes=[b_c])
                k.op(k.act, lambda e: e.activation(out=c_[:], in_=c_[:], func=AF.Silu), reads=[], writes=[b_c])
                cvs.append((c_, b_c))
            nb = []
            for a in range(2):
                c_, b_c = cvs[a]
                sq, b_sq = sqb.next()
                k.op(k.act, lambda e: e.activation(out=sq[:], in_=c_[:], func=AF.Square), reads=[b_c], writes=[b_sq])
                pb, b_pb = S.rot.next()
                k.op(k.pe, lambda e: e.matmul(pb[:, :], lhsT=S.ones_bf[:], rhs=sq[:], start=True, stop=True), reads=[b_sq, S.b_mc], writes=[b_pb])
                rs, b_rs = rsd.next()
                k.op(k.act, lambda e: e.activation(out=rs[:], in_=pb[:, :], func=AF.Sqrt, bias=S.eps_col[:, 0:1]), reads=[b_pb, S.b_const], writes=[b_rs])
                k.op(k.dve, lambda e: e.reciprocal(out=rs[:], in_=rs[:]), reads=[], writes=[b_rs])
                k.op(k.dve, lambda e: e.scalar_tensor_tensor(out=c_[:], in0=c_[:], scalar=(QSCALE if a == 0 else 1.0), in1=rs[:],
                                                              op0=ALU.mult, op1=ALU.mult), reads=[b_rs], writes=[b_c])
                nbt, b_nbt = (qnb if a == 0 else knb).next()
                k.op(k.pool, lambda e: e.tensor_copy(out=nbt[:], in_=c_[:]), reads=[b_c], writes=[b_nbt])
                nb.append((nbt, b_nbt))
            (qn, b_qn), (kn, b_kn), (vc, b_vc) = cvs
            (qnbt, b_qnbt), (knbt, b_knbt) = nb
            if DSTOP <= 0:
                continue
            a_, b_a, s_a = ra.next()
            bb_, b_bb, s_bb = rb.next()
            if DROWS < 1:
                continue
            k.dma(k.sp, s_a, a_[:], AB[h:h + 1, t0:t0 + TT], reads=[b_pr], writes=[b_a])
            if DROWS < 2:
                continue
            k.dma(k.sp, s_bb, bb_[:], AB[4 + h:5 + h, t0:t0 + TT], reads=[b_pr], writes=[b_bb])
            if DROWS < 3:
                continue
            k.op(k.act, lambda e: e.activation(out=bb_[:], in_=bb_[:], func=AF.Sigmoid), reads=[], writes=[b_bb])
            if DROWS < 4:
                continue
            k.op(k.act, lambda e: e.activation(out=a_[:], in_=a_[:], func=AF.Exp, bias=dtb[0:1, h:h + 1]), reads=[b_dc], writes=[b_a])
            if DROWS < 5:
                continue
            k.op(k.act, lambda e: e.activation(out=a_[:], in_=a_[:], func=AF.Ln, bias=S.one_col[0:1, 0:1]), reads=[S.b_mc], writes=[b_a])
            if DROWS < 6:
                continue
            k.op(k.dve, lambda e: e.tensor_scalar(out=a_[:], in0=a_[:], scalar1=alog[0:1, h:h + 1], scalar2=None, op0=ALU.mult),
                 reads=[b_dc], writes=[b_a])
            g_, b_g = rg.next()
            if DROWS < 7:
                continue
            k.op(k.dve, lambda e: e.tensor_tensor_scan(out=g_[:], data0=scan64[:], data1=a_[:], initial=0.0, op0=ALU.mult, op1=ALU.add),
                 reads=[b_a, b_dc], writes=[b_g])
            ed_, b_ed = red.next()
            gl = g_[:, CD - 1:TT:CD].unsqueeze(2).broadcast_to([1, TT // CD, CD])
            if DROWS < 8:
                continue
            k.op(k.dve, lambda e: e.tensor_tensor(out=ed_[:].rearrange("p (c t) -> p c t", t=CD), in0=gl,
                                                  in1=g_[:].rearrange("p (c t) -> p c t", t=CD), op=ALU.subtract), reads=[b_g], writes=[b_ed])
            if DROWS < 9:
                continue
            k.op(k.act, lambda e: e.activation(out=ed_[:], in_=ed_[:], func=AF.Exp), reads=[], writes=[b_ed])
            beg, b_beg = rbeg.next()
            if DROWS < 10:
                continue
            k.op(k.act, lambda e: e.activation(out=beg[:], in_=g_[:], func=AF.Exp), reads=[b_g], writes=[b_beg])
            if DROWS < 11:
                continue
            k.op(k.dve, lambda e: e.tensor_tensor(out=beg[:], in0=beg[:], in1=bb_[:], op=ALU.mult), reads=[b_bb], writes=[b_beg])
            pg, b_pg = S.rot.next()
            if DROWS < 12:
                continue
            k.op(k.pe, lambda e: e.matmul(pg[:, :], lhsT=ones_row[:], rhs=g_[:], start=True, stop=True), reads=[b_g, b_dc], writes=[b_pg])
            Gb_, b_Gb = Gb.next()
            Eb_, b_Eb = Eb.next()
            if DROWS < 13:
                continue
            k.op(k.dve, lambda e: e.tensor_copy(out=Gb_[:], in_=pg[:, :]), reads=[b_pg], writes=[b_Gb])
            if DROWS < 14:
                continue
            k.op(k.act, lambda e: e.activation(out=Eb_[:], in_=pg[:, :], func=AF.Exp), reads=[b_pg], writes=[b_Eb])
            qg, b_qg = qgb.next()
            if DROWS < 15:
                continue
            k.op(k.pool, lambda e: e.tensor_tensor(out=qg[:], in0=qn[:], in1=Eb_[:], op=ALU.mult), reads=[b_qn, b_Eb], writes=[b_qg])
            O, b_O = S.acc.next()
            for c in range(TT // CD):
                if DSTOP <= 1:
                    continue
                cs = slice(c * CD, (c + 1) * CD)
                pc, b_pc = S.rot.next()
                k.group(k.pe, [
                    (lambda e, j=j, r=r: e.matmul(pc[0:CD, 2 * j:2 * j + 2], lhsT=r[0:1, cs], rhs=ones_row[0:1, 0:2], start=True, stop=True))
                    for j, r in enumerate([g_, bb_, beg, ed_])], reads=[b_g, b_bb, b_beg, b_ed, b_dc], writes=[b_pc])
                col, b_col = colS.next()
                k.op(k.dve, lambda e: e.tensor_copy(out=col[:], in_=pc[0:CD, 0:8:2]), reads=[b_pc], writes=[b_col])
                r1, b_r1 = rel.next()
                k.op(k.dve, lambda e: e.tensor_scalar(out=r1[:], in0=Gb_[0:CD, cs], scalar1=-1.0, scalar2=col[:, 0:1], op0=ALU.mult, op1=ALU.add),
                     reads=[b_Gb, b_col], writes=[b_r1])
                k.op(k.pool, lambda e: e.tensor_tensor(out=r1[:], in0=r1[:], in1=maskSL[:], op=ALU.mult), reads=[b_dc], writes=[b_r1])
                k.op(k.act, lambda e: e.activation(out=r1[:], in_=r1[:], func=AF.Exp), reads=[], writes=[b_r1])
                r2, b_r2 = relT.next()
                k.op(k.dve, lambda e: e.tensor_scalar(out=r2[:], in0=Gb_[0:CD, cs], scalar1=col[:, 0:1], scalar2=None, op0=ALU.subtract),
                     reads=[b_Gb, b_col], writes=[b_r2])
                k.op(k.pool, lambda e: e.tensor_tensor(out=r2[:], in0=r2[:], in1=maskUI[:], op=ALU.mult), reads=[b_dc], writes=[b_r2])
                k.op(k.act, lambda e: e.activation(out=r2[:], in_=r2[:], func=AF.Exp), reads=[], writes=[b_r2])
                pk, b_pk = S.rot.next()
                k.group(k.pe, [
                    lambda e: e.matmul(pk[0:CD, 0:CD], lhsT=knbt[:, cs], rhs=knbt[:, cs], start=True, stop=True),
                    lambda e: e.matmul(pk[0:CD, 64:64 + CD], lhsT=knbt[:, cs], rhs=qnbt[:, cs], start=True, stop=True),
                ], reads=[b_knbt, b_qnbt], writes=[b_pk])
                X, b_X = Nm.next()
                k.op(k.dve, lambda e: e.scalar_tensor_tensor(out=X[:], in0=pk[0:CD, 0:CD], scalar=col[:, 1:2], in1=r1[:], op0=ALU.mult, op1=ALU.mult),
                     reads=[b_pk, b_col, b_r1], writes=[b_X])
                k.op(k.pool, lambda e: e.tensor_tensor(out=X[:], in0=X[:], in1=maskSL[:], op=ALU.mult), reads=[b_dc], writes=[b_X])
                qk, b_qk = QKm.next()
                k.op(k.dve, lambda e: e.tensor_tensor(out=r2[:], in0=pk[0:CD, 64:64 + CD], in1=r2[:], op=ALU.mult), reads=[b_pk], writes=[b_r2])
                k.op(k.pool, lambda e: e.tensor_tensor(out=qk[:], in0=r2[:], in1=maskUI[:], op=ALU.mult), reads=[b_r2, b_dc], writes=[b_qk])
                if DSTOP <= 2:
                    continue
                pn, b_pn = S.rot.next()
                k.op(k.pe, lambda e: e.transpose(out=pn[0:CD, 0:CD], in_=X[:], identity=id64[:]), reads=[b_X, b_dc], writes=[b_pn])
                XT, b_XT = Nm.next()
                k.op(k.act, lambda e: e.activation(out=XT[:], in_=pn[0:CD, 0:CD], func=AF.Copy), reads=[b_pn], writes=[b_XT])
                Tt, b_Tt = TTr.next()
                k.op(k.dve, lambda e: e.tensor_tensor(out=Tt[:], in0=id64[:], in1=XT[:], op=ALU.subtract), reads=[b_XT, b_dc], writes=[b_Tt])
                for lev in range(5):
                    last = lev == 4
                    p2, b_p2 = S.rot.next()
                    fns = [lambda e: e.matmul(p2[0:CD, 0:CD], lhsT=XT[:], rhs=X[:], start=True, stop=True)]
                    if not last:
                        fns.append(lambda e: e.matmul(p2[0:CD, 64:64 + CD], lhsT=X[:], rhs=XT[:], start=True, stop=True))
                    k.group(k.pe, fns, reads=[b_X, b_XT], writes=[b_p2])
                    X2, b_X2 = Nm.next()
                    k.op(k.act, lambda e: e.activation(out=X2[:], in_=p2[0:CD, 0:CD], func=AF.Copy), reads=[b_p2], writes=[b_X2])
                    if not last:
                        X2T, b_X2T = Nm.next()
                        k.op(k.dve, lambda e: e.tensor_copy(out=X2T[:], in_=p2[0:CD, 64:64 + CD]), reads=[b_p2], writes=[b_X2T])
                    p3, b_p3 = S.rot.next()
                    k.op(k.pe, lambda e: e.matmul(p3[0:CD, 0:CD], lhsT=X2[:], rhs=Tt[:], start=True, stop=True), reads=[b_X2, b_Tt], writes=[b_p3])
                    Tn, b_Tn = TTr.next()
                    k.op(k.dve, lambda e: e.tensor_tensor(out=Tn[:], in0=p3[0:CD, 0:CD], in1=Tt[:], op=ALU.add), reads=[b_p3, b_Tt], writes=[b_Tn])
                    Tt, b_Tt = Tn, b_Tn
                    if not last:
                        X, b_X, XT, b_XT = X2, b_X2, X2T, b_X2T
                if DSTOP <= 3:
                    continue
                ptk, b_ptk = S.rot.next()
                k.group(k.pe, [
                    lambda e: e.transpose(out=ptk[0:CD, 0:128], in_=kn[:, cs], identity=S.ident_f[:]),
                    lambda e: e.transpose(out=ptk[0:CD, 128:256], in_=vc[:, cs], identity=S.ident_f[:]),
                ], reads=[b_kn, b_vc, S.b_const], writes=[b_ptk])
                kt_, b_kt = ktok.next()
                kd_, b_kd = kdb.next()
                bv_, b_bv = bv.next()
                k.op(k.dve, lambda e: e.tensor_scalar(out=kt_[:], in0=ptk[0:CD, 0:128], scalar1=col[:, 2:3], scalar2=None, op0=ALU.mult),
                     reads=[b_ptk, b_col], writes=[b_kt])
                k.op(k.act, lambda e: e.activation(out=kd_[:], in_=ptk[0:CD, 0:128], func=AF.Copy, scale=col[:, 3:4]),
                     reads=[b_ptk, b_col], writes=[b_kd])
                k.op(k.dve, lambda e: e.tensor_scalar(out=bv_[:], in0=ptk[0:CD, 128:256], scalar1=col[:, 1:2], scalar2=None, op0=ALU.mult),
                     reads=[b_ptk, b_col], writes=[b_bv])
                pu, b_pu = S.rot.next()
                k.group(k.pe, [
                    lambda e: e.matmul(pu[0:CD, 0:128], lhsT=Tt[:], rhs=bv_[:], start=True, stop=True),
                ], reads=[b_Tt, b_bv], writes=[b_pu])
                pw, b_pw = S.rot.next()
                k.op(k.pe, lambda e: e.matmul(pw[:, 0:CD], lhsT=kt_[:], rhs=Tt[:], start=True, stop=True), reads=[b_kt, b_Tt], writes=[b_pw])
                u_, b_u = ut.next()
                wT, b_wT = wTb.next()
                k.op(k.act, lambda e: e.activation(out=u_[:], in_=pu[0:CD, 0:128], func=AF.Copy), reads=[b_pu], writes=[b_u])
                k.op(k.dve, lambda e: e.tensor_copy(out=wT[:], in_=pw[:, 0:CD]), reads=[b_pw], writes=[b_wT])
                if DSTOP <= 4:
                    continue
                pws, b_pws = S.rot.next()
                k.op(k.pe, lambda e: e.matmul(pws[0:CD, 0:128], lhsT=wT[:], rhs=sb_cur[:], start=True, stop=True), reads=[b_wT, b_sb_cur], writes=[b_pws])
                vn, b_vn = vnew.next()
                k.op(k.dve, lambda e: e.tensor_tensor(out=vn[:], in0=u_[:], in1=pws[0:CD, 0:128], op=ALU.subtract), reads=[b_u, b_pws], writes=[b_vn])
                k.group(k.pe, [
                    lambda e: e.matmul(O[:, cs], lhsT=sb_cur[:], rhs=qg[:, cs], start=True, stop=False),
                    lambda e: e.matmul(O[:, cs], lhsT=vn[:], rhs=qk[:], start=False, stop=True),
                ], reads=[b_sb_cur, b_qg, b_vn, b_qk], writes=[b_O])
                psu, b_psu = S.rot.next()
                k.op(k.pe, lambda e: e.matmul(psu[:, 0:128], lhsT=kd_[:], rhs=vn[:], start=True, stop=True), reads=[b_kd, b_vn], writes=[b_psu])
                k.op(k.dve, lambda e: e.scalar_tensor_tensor(out=Sf[:], in0=Sf[:], scalar=Eb_[:, c * CD + CD - 1:c * CD + CD], in1=psu[:, 0:128],
                                                              op0=ALU.mult, op1=ALU.add), reads=[b_psu, b_Eb], writes=[b_Sf])
                sb_cur, b_sb_cur = Sb.next()
                k.op(k.act, lambda e: e.activation(out=sb_cur[:], in_=Sf[:], func=AF.Copy), reads=[b_Sf], writes=[b_sb_cur])
            if DSTOP <= 4:
                continue
            oT, b_oT = S.oT.next()
            k.op(k.act, lambda e: e.activation(out=oT[:], in_=O[:, :], func=AF.Copy), reads=[b_O], writes=[b_oT])
            evs.append(head_post(S, oT[:], b_oT, 12 + h, t0, TT, mixedT, b_mixed, gate_ap=FMf[24 + h, :, t0:t0 + TT], b_gate=b_pr))
    return evs


MODBLK = N_MOD * D_MODEL // 512


def build_program(S_len, depth):
    import os
    FSTOP = int(os.environ.get('FSTOP', '99'))
    nt = S_len // TT
    nc = bass.Bass("TRN2", target_bir_lowering=False)

    def din(name, shape, dt=F32):
        return nc.dram_tensor(name, shape, dt, kind="ExternalInput").ap()

    def dint(name, shape, dt):
        return nc.dram_tensor(name, shape, dt, kind="Internal").ap()

    x = din("x", [S_len, D_MODEL])
    cT = din("cT", [128, KC])
    wmod = din("wmod", [depth, MODBLK, 128, KC * 512])
    bmod = din("bmod", [depth, 1, N_MOD * D_MODEL])
    gain = din("gain", [depth, 6, D_MODEL])
    w13a = din("w13a", [depth, NFF, 128, KC * 256])
    w2a = din("w2a", [depth, 4, NFF // 4, 128, 2048])
    w13b = din("w13b", [depth, NFF, 128, KC * 256])
    w2b = din("w2b", [depth, 4, NFF // 4, 128, 2048])
    wfm = din("wfm", [depth, 22, 128, KC * 256])
    wtm = din("wtm", [depth, 3, 4, 128, 2048])
    wab = din("wab", [depth, 128, KC * 128])
    wout = din("wout", [depth, 4, 4, 128, 2048])
    og = din("og", [depth, 4, 512])
    ident = din("ident", [128, 128])
    mc = mixer_consts()
    cst = {n: din("c_" + n, list(v.shape), BF16) for n, v in mc.items()}
    cdc = {n: din(n, ([depth] + sh) if n in ("cwP", "alog", "dtb") else sh, dt) for n, (sh, dt) in CD_SHAPES.items()}
    out = nc.dram_tensor("out", [S_len, D_MODEL], F32, kind="ExternalOutput").ap()

    xs = dint("xs", [S_len, D_MODEL], F32)
    modD = dint("modD", [depth, N_MOD, D_MODEL], F32)
    FMb = dint("FMb", [NFMB, 128, S_len], BF16)
    FMf = dint("FMf", [NFMF, 128, S_len], F32)
    TMv = dint("TMv", [3, S_len, 512], BF16)
    AB = dint("AB", [8, S_len], F32)
    mixedT = dint("mixedT", [D_MODEL, S_len], BF16)
    s_w13a = dint("s_w13a", [NFF, 128, KC * 256], BF16)
    s_w2a = dint("s_w2a", [4, NFF // 4, 128, 2048], BF16)
    s_w13b = dint("s_w13b", [NFF, 128, KC * 256], BF16)
    s_w2b = dint("s_w2b", [4, NFF // 4, 128, 2048], BF16)
    s_wfm = dint("s_wfm", [22, 128, KC * 256], BF16)
    s_wtm = dint("s_wtm", [3, 4, 128, 2048], BF16)
    s_wout = dint("s_wout", [4, 4, 128, 2048], BF16)

    def flat(ap):
        nd = len(ap.shape)
        names = " ".join(f"d{i}" for i in range(nd))
        return ap.rearrange(f"{names} -> ({names})")

    with contextlib.ExitStack() as es:
        k = K(nc, es)
        S = setup_common(k, nc, es, {"ident": ident[:, :]})
        S.eps_col = k.sb("eps_col", [128, 1], F32)
        k.op(k.dve, lambda e: e.memset(S.eps_col[:], EPS), writes=[S.b_const])
        b_modD, b_xs, b_out, b_pr, b_mixed = Buf("modD"), Buf("xs"), Buf("out"), Buf("pr"), Buf("mixed")
        bw = {n: Buf(n) for n in ["w13a", "w2a", "w13b", "w2b", "wfm", "wtm", "wout"]}
        s_cast = k.slot("cast")
        b_xsl = [[Buf(f"xs{t}_{tb}") for tb in range(TT // 128)] for t in range(nt)]
        s_fin = k.slot("fin")

        def fin():
            fe = [k.dma(k.sp, s_fin, out[t * TT:(t + 1) * TT, :], xs[t * TT:(t + 1) * TT, :], reads=b_xsl[t], writes=[b_out]) for t in range(nt)]
            k.finish(fe)
            return nc, mc

        with contextlib.ExitStack() as tes:
            k.enter(tes)
            ct = k.sb("ct", [128, KC], F32)
            b_ct = Buf("ct")
            s_ct = k.slot("ct")
            k.dma(k.sp, s_ct, ct[:], cT[:, :], writes=[b_ct])
            k.op(k.act, lambda e: e.activation(out=ct[:], in_=ct[:], func=AF.Silu), writes=[b_ct])
            wr = Ring([(k.sb(f"wm{i}", [128, KC * 512], F32), Buf(f"wm{i}"), k.slot(f"wm{i}")) for i in range(2)])
            br = Ring([(k.sb(f"bmr{i}", [1, 512], F32), Buf(f"bmr{i}"), k.slot(f"bmr{i}")) for i in range(2)])
            rr = Ring([(k.sb(f"mres{i}", [1, 512], F32), Buf(f"mres{i}"), k.slot(f"mres{i}")) for i in range(2)])
            modflat = modD.rearrange("l j d -> l (j d)")
            for l in range(depth):
                for blk in range(MODBLK):
                    wt, b_wt, s_wt = wr.next()
                    bt_, b_bt, s_bt = br.next()
                    rt, b_rt, s_rt = rr.next()
                    k.dma(k.sp, s_wt, wt[:], wmod[l, blk], writes=[b_wt])
                    k.dma(k.sp, s_bt, bt_[:], bmod[l, 0:1, blk * 512:(blk + 1) * 512], writes=[b_bt])
                    pb, b_pb = bank(S)
                    k.group(k.pe, [(lambda e, kc=kc: e.matmul(pb[0:1, :], lhsT=ct[:, kc:kc + 1], rhs=wt[:, kc * 512:(kc + 1) * 512],
                                                              start=(kc == 0), stop=(kc == KC - 1))) for kc in range(KC)],
                            reads=[b_ct, b_wt], writes=[b_pb])
                    k.op(k.dve, lambda e: e.tensor_tensor(out=rt[:], in0=pb[0:1, :], in1=bt_[:], op=ALU.add), reads=[b_pb, b_bt], writes=[b_rt])
                    k.dma(k.pool, s_rt, modflat[l:l + 1, blk * 512:(blk + 1) * 512], rt[:], reads=[b_rt], writes=[b_modD])
            k.reset()
            k.leave(es)
        if FSTOP <= 1:
            return nc, mc

        for l in range(depth):
            for nme, dst, src in [("w13a", s_w13a, w13a[l]), ("w2a", s_w2a, w2a[l]), ("wfm", s_wfm, wfm[l]), ("wtm", s_wtm, wtm[l]),
                                  ("wout", s_wout, wout[l]), ("w13b", s_w13b, w13b[l]), ("w2b", s_w2b, w2b[l])]:
                n_el = int(np.prod(dst.shape))
                cast_copy(k, s_cast, flat(dst), flat(src), n_el, [], [bw[nme]])

            def ffn_stage(first, w13s, b_w13, w2s, b_w2, n_pre, j0, n_post, src, b_src, dst, b_dst):
                set_stage_coefs(S, modD[l], gain[l], b_modD, n_pre=n_pre, j_sh=j0, j_sc=j0 + 1, j_g=j0 + 2, n_post=n_post, gscale=0.5)
                evs = []
                for t in range(nt):
                    t0 = t * TT
                    rows_in = lambda tb, t0=t0: src[t0 + tb * 128: t0 + (tb + 1) * 128, :]
                    rows_out = lambda tb, t0=t0: dst[t0 + tb * 128: t0 + (tb + 1) * 128, :]
                    bs_ = b_src[t] if isinstance(b_src, list) else b_src
                    bd_ = b_dst[t] if isinstance(b_dst, list) else b_dst
                    prep_tile(S, rows_in, bs_)
                    ffn_tile(S, lambda j: w13s[j], b_w13, lambda cb, jg: w2s[cb, jg], b_w2)
                    evs += residual_epilogue(S, rows_in, bs_, rows_out, bd_, store_q=(k.sp if dst is out else None))
                return evs

            with contextlib.ExitStack() as tes:
                k.enter(tes)
                setup_row(S, tes)
                setup_proj(S, tes)
                load_mod(S, modD[l], gain[l], b_modD)
                k.dma(k.pool, S.s_wab, S.wab[:], wab[l], writes=[S.b_wab])
                src, b_src = (x, Buf("xin")) if l == 0 else (xs, b_xsl)
                ffn_stage(True, s_w13a, bw["w13a"], s_w2a, bw["w2a"], 0, 0, 1, src, b_src, xs, b_xsl)
                k.reset()
                if FSTOP <= 2:
                    return fin()
                set_stage_coefs(S, modD[l], gain[l], b_modD, n_pre=2, j_sh=3, j_sc=4, j_g=None, n_post=None, gscale=1.0)
                for t in range(nt):
                    t0 = t * TT
                    prep_tile(S, lambda tb, t0=t0: xs[t0 + tb * 128: t0 + (tb + 1) * 128, :], b_xsl[t])
                    proj_tile(S, t0, lambda pr: s_wfm[pr], bw["wfm"], lambda cb, jg: s_wtm[cb, jg], bw["wtm"], FMb, FMf, TMv, AB, b_pr)
                k.reset()
                k.leave(es)
            if FSTOP <= 3:
                return nc, mc

            with contextlib.ExitStack() as tes:
                k.enter(tes)
                setup_mix_common(S, cst, og[l], S_len)
                with contextlib.ExitStack() as tes2:
                    k.enter(tes2)
                    setup_attn(S, cst)
                    S.acc = Pool2(S, [0, 1, 2, 3])
                    S.rot = Pool2(S, [4, 5, 6, 7])
                    for h in range(GH):
                        load_head_qkv(S, FMb, h, 4 + h, TMv, 0, h, b_pr)
                        mixer_A_head(S, h, mixedT, b_mixed)
                    k.reset()
                    S.acc = Pool2(S, [0, 1])
                    S.rot = Pool2(S, [2, 3, 4, 5, 6, 7])
                    for h in range(GH):
                        load_head_qkv(S, FMb, 8 + h, 12 + h, TMv, 1, h, b_pr)
                        mixer_B_head(S, 4 + h, mixedT, b_mixed)
                        k.reset()
                    k.leave(tes)
                if FSTOP <= 4:
                    return nc, mc
                ex = {n: (cdc[n][l] if n in ("cwP", "alog", "dtb") else cdc[n]) for n in cdc}
                with contextlib.ExitStack() as tes2:
                    k.enter(tes2)
                    mixer_C(S, FMf, TMv, ex, mixedT, b_pr, b_mixed, l)
                    k.reset()
                    k.leave(tes)
                if FSTOP <= 5:
                    return nc, mc
                with contextlib.ExitStack() as tes2:
                    k.enter(tes2)
                    mixer_D(S, FMf, AB, ex, mixedT, b_pr, b_mixed)
                    k.reset()
                    k.leave(tes)
                k.leave(es)
            if FSTOP <= 6:
                fe = [k.dma(k.sp, s_fin, out[0:128, 0:S_len], FMf[4], reads=[b_pr], writes=[b_out]),
                      k.dma(k.pool, s_fin, out[128:256, 0:S_len], mixedT[1024:1152, :], reads=[b_mixed], writes=[b_out]),
                      k.dma(k.pool, s_fin, out[256:384, 0:S_len], mixedT[0:128, :], reads=[b_mixed], writes=[b_out])]
                k.finish(fe)
                return nc, mc

            with contextlib.ExitStack() as tes:
                k.enter(tes)
                setup_row(S, tes)
                load_mod(S, modD[l], gain[l], b_modD)
                set_stage_coefs(S, modD[l], gain[l], b_modD, n_pre=2, j_sh=3, j_sc=4, j_g=5, n_post=3, gscale=1.0)
                s_mx = k.slot("mx")
                b_hl = [S.b_hT] * KC
                for t in range(nt):
                    t0 = t * TT
                    k.dma(k.sp, s_mx, S.hT[:].rearrange("p (c t) -> p c t", t=TT),
                          mixedT[:, t0:t0 + TT].rearrange("(c p) t -> p c t", p=128), reads=[b_mixed], writes=[S.b_hT])
                    mm_tokmajor(S, lambda j: S.hT[:, j * TT:(j + 1) * TT], b_hl, KC, lambda cb, jg: s_wout[cb, jg], bw["wout"],
                                lambda tb, cb: S.y[:, tb * D_MODEL + cb * 512: tb * D_MODEL + (cb + 1) * 512], S.b_y)
                    rows = lambda tb, t0=t0: xs[t0 + tb * 128: t0 + (tb + 1) * 128, :]
                    residual_epilogue(S, rows, b_xsl[t], rows, b_xsl[t])
                k.reset()
                if FSTOP <= 7:
                    return fin()
                last = l == depth - 1
                dst, b_dst = (xs, b_xsl)
                evs = ffn_stage(False, s_w13b, bw["w13b"], s_w2b, bw["w2b"], 4, 6, 5, xs, b_xsl, dst, b_dst)
                k.reset()
                if last:
                    fin()
                k.leave(es)
        print("program: ninst", k.ninst, "ndma", k.ndma, "nsem", k.nsem, flush=True)
    return nc, mc


def host_inputs(depth, c_b, inputs, mc):
    f = {}
    wm = inputs["w_mod"][:depth]
    f["wmod"] = np.ascontiguousarray(wm.reshape(depth, KC, 128, MODBLK, 512).transpose(0, 3, 2, 1, 4).reshape(depth, MODBLK, 128, KC * 512))
    f["bmod"] = np.ascontiguousarray(inputs["b_mod"][:depth].reshape(depth, 1, N_MOD * D_MODEL))
    f["gain"] = np.ascontiguousarray(inputs["norm_gain"][:depth])
    f["w13a"] = np.stack([lay_w13(inputs["ffn1_w13"][l]) for l in range(depth)])
    f["w2a"] = np.stack([lay_w2(inputs["ffn1_w2"][l], NFF) for l in range(depth)])
    f["w13b"] = np.stack([lay_w13(inputs["ffn2_w13"][l]) for l in range(depth)])
    f["w2b"] = np.stack([lay_w2(inputs["ffn2_w2"][l], NFF) for l in range(depth)])
    wi = [lay_win(inputs["w_in"][l]) for l in range(depth)]
    f["wfm"] = np.stack([w[0] for w in wi])
    f["wtm"] = np.stack([w[1] for w in wi])
    f["wab"] = np.stack([w[2] for w in wi])
    f["wout"] = np.stack([lay_w2(inputs["w_out"][l], KC) for l in range(depth)])
    f["og"] = np.ascontiguousarray(inputs["mix_out_gain"][:depth])
    f["ident"] = np.eye(128, dtype=np.float32)
    for n, v in mc.items():
        f["c_" + n] = v
    cds = [mixer_cd_host(inputs["hgrn_lb_logits"], inputs["dn_conv_w"][l], inputs["dn_a_log"][l], inputs["dn_dt_bias"][l]) for l in range(depth)]
    for n in CD_SHAPES:
        if n in ("cwP", "alog", "dtb"):
            f[n] = np.stack([cd[n] for cd in cds])
        else:
            f[n] = cds[0][n]
    return f


def kernel(**inputs):
    inputs = {k_: np.asarray(v) for k_, v in inputs.items()}
    nc, mc = build_program(SEQ, DEPTH)
    shared = host_inputs(DEPTH, None, inputs, mc)
    in_maps = []
    for b in range(BATCH):
        m = dict(shared)
        m["x"] = np.ascontiguousarray(inputs["x"][b])
        m["cT"] = np.ascontiguousarray(inputs["c"][b].reshape(KC, 128).T)
        in_maps.append(m)
    res = run_bass_kernel_spmd(nc, in_maps, core_ids=list(range(BATCH)))
    return np.stack([r["out"] for r in res.results], axis=0).astype(np.float32)
```
